# Optimizing a Trainium2 kernel written in Bass

```python
import jax, jax.numpy as jnp
from jax import lax
import numpy as np

D_MODEL = 1024
BATCH = 32
SEQ = 2048
DEPTH = 2

GRID_W = 64
CTX_LEN = 256
HEAD_DIM = 64
AXIS_DIM = HEAD_DIM // 2
ROPE_THETA = 10000.0
A_HEADS = 6
A_KV = 2
A_GROUP = A_HEADS // A_KV
C_HEADS = 6
C_KV = 2
C_GROUP = C_HEADS // C_KV
POOL_GROUPS = 4
POOL_CH = 64
POOL_WIDTH = POOL_GROUPS * POOL_CH
POOL_WINDOWS = (2, 4, 8, 16)
WINDOW = 128
Q_BLOCK = 128
BAND = Q_BLOCK + 2 * WINDOW
D_FF = 4 * D_MODEL
N_BRANCH = 3
A_QW = A_HEADS * HEAD_DIM
A_KVW = A_KV * HEAD_DIM
C_QW = C_HEADS * HEAD_DIM
C_KVW = C_KV * HEAD_DIM
IN_SPLITS = (A_QW, A_KVW, A_KVW, C_QW, C_KVW, C_KVW, POOL_WIDTH, D_MODEL, D_MODEL, D_MODEL)
IN_WIDTH = A_QW + 2 * A_KVW + C_QW + 2 * C_KVW + POOL_WIDTH + N_BRANCH * D_MODEL
EPS = 1e-6
NEG = -1e30

kernel_name = "hybrid_prefix_gqa_pool_window_block"


def rmsnorm(x, g):
    xf = x.astype(jnp.float32)
    y = xf * lax.rsqrt(jnp.mean(xf * xf, axis=-1, keepdims=True) + EPS)
    return (y * g.astype(jnp.float32)).astype(x.dtype)


def modulate(x, g, shift, scale):
    return rmsnorm(x, g) * (1 + scale) + shift


def adaln(v, w, b):
    m = jax.nn.silu(v) @ w + b
    return jnp.split(m, 6, axis=-1)


def split_in(z):
    idx = []
    acc = 0
    for s in IN_SPLITS[:-1]:
        acc += s
        idx.append(acc)
    return jnp.split(z, idx, axis=-1)


def heads_q(t, n_kv, n_group):
    b, l, _ = t.shape
    return t.reshape(b, l, n_kv, n_group, HEAD_DIM)


def heads_kv(t, n_kv):
    b, l, _ = t.shape
    return t.reshape(b, l, n_kv, HEAD_DIM)


def rope_tables(n_tok):
    rows = n_tok // GRID_W
    r = jnp.repeat(jnp.arange(rows, dtype=jnp.float32), GRID_W)
    col = jnp.tile(jnp.arange(GRID_W, dtype=jnp.float32), rows)
    inv = 1.0 / (ROPE_THETA ** (jnp.arange(0, AXIS_DIM, 2, dtype=jnp.float32) / AXIS_DIM))
    ang = jnp.concatenate([r[:, None] * inv, col[:, None] * inv], axis=-1)
    return jnp.cos(ang), jnp.sin(ang)


def apply_rope(x, cos, sin):
    shp = x.shape
    xr = x.reshape(shp[:-1] + (shp[-1] // 2, 2))
    x0, x1 = xr[..., 0], xr[..., 1]
    bshape = (shp[1],) + (1,) * (x.ndim - 3) + (shp[-1] // 2,)
    cs = cos.reshape(bshape).astype(x.dtype)
    sn = sin.reshape(bshape).astype(x.dtype)
    return jnp.stack([x0 * cs - x1 * sn, x0 * sn + x1 * cs], axis=-1).reshape(shp)


def global_attn(q, k, v):
    b, s, hk, g, dh = q.shape
    nblk = s // Q_BLOCK
    qb = q.reshape(b, nblk, Q_BLOCK, hk, g, dh).transpose(1, 0, 2, 3, 4, 5)
    scale = dh ** -0.5

    def one_block(qi):
        sc = jnp.einsum('bqhgd,bkhd->bhgqk', qi, k).astype(jnp.float32) * scale
        p = jax.nn.softmax(sc, axis=-1).astype(v.dtype)
        return jnp.einsum('bhgqk,bkhd->bqhgd', p, v)

    o = lax.map(one_block, qb)
    return o.transpose(1, 0, 2, 3, 4, 5).reshape(b, s, hk * g * dh)


def window_attn(q, k, v, kc, vc, sink):
    b, s, hk, g, dh = q.shape
    nctx = kc.shape[1]
    nblk = s // Q_BLOCK
    scale = dh ** -0.5
    k_pad = jnp.pad(k, ((0, 0), (WINDOW, WINDOW), (0, 0), (0, 0)))
    v_pad = jnp.pad(v, ((0, 0), (WINDOW, WINDOW), (0, 0), (0, 0)))
    qb = q.reshape(b, nblk, Q_BLOCK, hk, g, dh).transpose(1, 0, 2, 3, 4, 5)
    sink_f = sink.astype(jnp.float32).reshape(1, hk, g, 1, 1)

    def one_block(args):
        qi, bi = args
        start = bi * Q_BLOCK
        kb = lax.dynamic_slice_in_dim(k_pad, start, BAND, axis=1)
        vb = lax.dynamic_slice_in_dim(v_pad, start, BAND, axis=1)
        qpos = start + jnp.arange(Q_BLOCK)
        kpos = start - WINDOW + jnp.arange(BAND)
        valid = (jnp.abs(qpos[:, None] - kpos[None, :]) <= WINDOW) & (kpos[None, :] >= 0) & (kpos[None, :] < s)
        s_loc = jnp.einsum('bqhgd,bkhd->bhgqk', qi, kb).astype(jnp.float32) * scale
        s_loc = jnp.where(valid, s_loc, NEG)
        s_ctx = jnp.einsum('bqhgd,bkhd->bhgqk', qi, kc).astype(jnp.float32) * scale
        s_sink = jnp.broadcast_to(sink_f, s_ctx.shape[:-1] + (1,))
        p = jax.nn.softmax(jnp.concatenate([s_ctx, s_loc, s_sink], axis=-1), axis=-1).astype(v.dtype)
        o = jnp.einsum('bhgqk,bkhd->bqhgd', p[..., :nctx], vc)
        return o + jnp.einsum('bhgqk,bkhd->bqhgd', p[..., nctx:nctx + BAND], vb)

    o = lax.map(one_block, (qb, jnp.arange(nblk)))
    return o.transpose(1, 0, 2, 3, 4, 5).reshape(b, s, hk * g * dh)


def ctx_attn(q, k, v, sink):
    b, n, hk, g, dh = q.shape
    sc = jnp.einsum('bqhgd,bkhd->bhgqk', q, k).astype(jnp.float32) * (dh ** -0.5)
    if sink is not None:
        sk = jnp.broadcast_to(sink.astype(jnp.float32).reshape(1, hk, g, 1, 1), sc.shape[:-1] + (1,))
        p = jax.nn.softmax(jnp.concatenate([sc, sk], axis=-1), axis=-1)[..., :-1]
    else:
        p = jax.nn.softmax(sc, axis=-1)
    o = jnp.einsum('bhgqk,bkhd->bqhgd', p.astype(v.dtype), v)
    return o.reshape(b, n, hk * g * dh)


def pool_mix(u, w_pool, pool_scale):
    b, n, _ = u.shape
    uf = u.astype(jnp.float32).reshape(b, n, POOL_GROUPS, POOL_CH)
    cs = jnp.concatenate([jnp.zeros((b, 1, POOL_GROUPS, POOL_CH), jnp.float32), jnp.cumsum(uf, axis=1)], axis=1)
    t = jnp.arange(n)
    outs = []
    for gi, w in enumerate(POOL_WINDOWS):
        lo = jnp.clip(t - w // 2, 0, n)
        hi = jnp.clip(t + w - w // 2, 0, n)
        tot = cs[:, hi, gi] - cs[:, lo, gi]
        cnt = (hi - lo).astype(jnp.float32)
        outs.append(tot / cnt[None, :, None] - uf[:, :, gi])
    pooled = jnp.stack(outs, axis=2).astype(u.dtype)
    mixed = jnp.einsum('blgc,gcd->blgd', pooled, w_pool).reshape(b, n, POOL_WIDTH)
    return mixed * pool_scale


def project(h, w_in, qn_a, kn_a, qn_c, kn_c):
    qa, ka, va, qc, kc, vc, u, ga, gb, gc = split_in(h @ w_in)
    qa = rmsnorm(heads_q(qa, A_KV, A_GROUP), qn_a)
    ka = rmsnorm(heads_kv(ka, A_KV), kn_a)
    va = heads_kv(va, A_KV)
    qc = rmsnorm(heads_q(qc, C_KV, C_GROUP), qn_c)
    kc = rmsnorm(heads_kv(kc, C_KV), kn_c)
    vc = heads_kv(vc, C_KV)
    return qa, ka, va, qc, kc, vc, u, ga, gb, gc


def merge(oa, ob, oc, ga, gb, gc, w_br_a, w_br_b, w_br_c, w_out):
    y = (jax.nn.sigmoid(ga) * (oa @ w_br_a)
         + jax.nn.sigmoid(gb) * (ob @ w_br_b)
         + jax.nn.sigmoid(gc) * (oc @ w_br_c))
    return y @ w_out


def mlp(h, w1, w2):
    return jnp.square(jax.nn.relu(h @ w1)) @ w2


def setup_inputs(seed: int = 0) -> dict:
    key = jax.random.key(seed)
    ks = jax.random.split(key, 24)
    f32 = jnp.float32
    nrm = lambda k, shp, s: jax.random.normal(k, shp, f32) * s
    return {
        "x": nrm(ks[0], (BATCH, SEQ, D_MODEL), 1.0),
        "c": nrm(ks[1], (BATCH, D_MODEL), 1.0),
        "ctx": nrm(ks[2], (BATCH, CTX_LEN, D_MODEL), 1.0),
        "c_ctx": nrm(ks[3], (D_MODEL,), 1.0),
        "w_ada": nrm(ks[4], (DEPTH, D_MODEL, 6 * D_MODEL), 0.5 * D_MODEL ** -0.5),
        "b_ada": nrm(ks[5], (DEPTH, 6 * D_MODEL), 0.02),
        "norm1": 1.0 + nrm(ks[6], (DEPTH, D_MODEL), 0.1),
        "norm2": 1.0 + nrm(ks[7], (DEPTH, D_MODEL), 0.1),
        "w_in": nrm(ks[8], (DEPTH, D_MODEL, IN_WIDTH), D_MODEL ** -0.5),
        "q_norm_a": 1.0 + nrm(ks[9], (DEPTH, HEAD_DIM), 0.1),
        "k_norm_a": 1.0 + nrm(ks[10], (DEPTH, HEAD_DIM), 0.1),
        "q_norm_c": 1.0 + nrm(ks[11], (DEPTH, HEAD_DIM), 0.1),
        "k_norm_c": 1.0 + nrm(ks[12], (DEPTH, HEAD_DIM), 0.1),
        "sink_c": nrm(ks[13], (DEPTH, C_HEADS), 0.5),
        "w_pool": nrm(ks[14], (DEPTH, POOL_GROUPS, POOL_CH, POOL_CH), POOL_CH ** -0.5),
        "pool_scale": 1.0 + nrm(ks[15], (DEPTH, POOL_WIDTH), 0.1),
        "w_br_a": nrm(ks[16], (DEPTH, A_QW, D_MODEL), A_QW ** -0.5),
        "w_br_b": nrm(ks[17], (DEPTH, POOL_WIDTH, D_MODEL), POOL_WIDTH ** -0.5),
        "w_br_c": nrm(ks[18], (DEPTH, C_QW, D_MODEL), C_QW ** -0.5),
        "w_out": nrm(ks[19], (DEPTH, D_MODEL, D_MODEL), D_MODEL ** -0.5),
        "w_mlp1": nrm(ks[20], (DEPTH, D_MODEL, D_FF), D_MODEL ** -0.5),
        "w_mlp2": nrm(ks[21], (DEPTH, D_FF, D_MODEL), D_FF ** -0.5),
    }


def reference(x, c, ctx, c_ctx, w_ada, b_ada, norm1, norm2, w_in, q_norm_a, k_norm_a, q_norm_c, k_norm_c,
              sink_c, w_pool, pool_scale, w_br_a, w_br_b, w_br_c, w_out, w_mlp1, w_mlp2):
    n_tok = x.shape[1]
    cos, sin = rope_tables(n_tok)
    xc = ctx
    for l in range(DEPTH):
        last = l == DEPTH - 1
        sh1, sc1, g1, sh2, sc2, g2 = [m[:, None, :] for m in adaln(c, w_ada[l], b_ada[l])]
        csh1, csc1, cg1, csh2, csc2, cg2 = adaln(c_ctx, w_ada[l], b_ada[l])

        hc = modulate(xc, norm1[l], csh1, csc1)
        cqa, cka, cva, cqc, ckc, cvc, cu, cga, cgb, cgc = project(
            hc, w_in[l], q_norm_a[l], k_norm_a[l], q_norm_c[l], k_norm_c[l])

        h = modulate(x, norm1[l], sh1, sc1)
        qa, ka, va, qc, kc, vc, u, ga, gb, gc = project(
            h, w_in[l], q_norm_a[l], k_norm_a[l], q_norm_c[l], k_norm_c[l])
        qa, ka = apply_rope(qa, cos, sin), apply_rope(ka, cos, sin)
        qc, kc = apply_rope(qc, cos, sin), apply_rope(kc, cos, sin)

        oa = global_attn(qa, jnp.concatenate([cka, ka], axis=1), jnp.concatenate([cva, va], axis=1))
        ob = pool_mix(u, w_pool[l], pool_scale[l])
        oc = window_attn(qc, kc, vc, ckc, cvc, sink_c[l])
        x = x + g1 * merge(oa, ob, oc, ga, gb, gc, w_br_a[l], w_br_b[l], w_br_c[l], w_out[l])
        x = x + g2 * mlp(modulate(x, norm2[l], sh2, sc2), w_mlp1[l], w_mlp2[l])

        if not last:
            coa = ctx_attn(cqa, cka, cva, None)
            cob = pool_mix(cu, w_pool[l], pool_scale[l])
            coc = ctx_attn(cqc, ckc, cvc, sink_c[l])
            xc = xc + cg1 * merge(coa, cob, coc, cga, cgb, cgc, w_br_a[l], w_br_b[l], w_br_c[l], w_out[l])
            xc = xc + cg2 * mlp(modulate(xc, norm2[l], csh2, csc2), w_mlp1[l], w_mlp2[l])
    return x
```

```python
import contextlib
import numpy as np
import ml_dtypes
import concourse.bass as bass
import concourse.mybir as mybir
from concourse.bass_utils import run_bass_kernel_spmd

F32 = mybir.dt.float32
BF16 = mybir.dt.bfloat16
AF = mybir.ActivationFunctionType
ALU = mybir.AluOpType

NDMA_SLOTS = 8
CTX = 256
EPS = 1e-6
NBLK_L = 116
NBLK_A = 48
NOCACHE = False
ILV = [True, True]


class Buf:
    __slots__ = ("name", "lw", "rd")

    def __init__(self, name):
        self.name = name
        self.lw = None
        self.rd = {}


class Op:
    __slots__ = ("stream", "fn", "deps", "signal", "is_dma", "semkey", "sigval")

    def __init__(self, stream, fn, is_dma):
        self.stream = stream
        self.fn = fn
        self.deps = []
        self.signal = False
        self.is_dma = is_dma
        self.semkey = None
        self.sigval = None


class Prog:
    def __init__(self, nc):
        self.nc = nc
        self.ops = []
        self.bufs = {}
        self.dma_count = {}
        self.dma_ops = {}
        self.streams = ("pe", "act", "dve", "pool", "sp")

    def B(self, *key):
        b = self.bufs.get(key)
        if b is None:
            b = Buf(key)
            self.bufs[key] = b
        return b

    def _add(self, op, reads, writes):
        deps = {}

        def need(p, raw):
            if p is None:
                return
            if (not p.is_dma) and (not op.is_dma) and p.stream == op.stream:
                if op.stream == "pe":
                    return
            deps[id(p)] = p

        for r in reads:
            need(r.lw, True)
        for w in writes:
            need(w.lw, False)
            for q in w.rd.values():
                need(q, False)
        op.deps = list(deps.values())
        for q in op.deps:
            q.signal = True
        for r in reads:
            r.rd[id(op) if op.is_dma else op.stream] = op
        for w in writes:
            w.lw = op
            w.rd = {}
        self.ops.append(op)
        return op

    def op(self, stream, fn, reads=(), writes=()):
        return self._add(Op(stream, fn, False), reads, writes)

    def dma(self, stream, fn, reads=(), writes=()):
        op = Op(stream, fn, True)
        n = self.dma_count.get(stream, 0)
        self.dma_count[stream] = n + 1
        op.semkey = ("dma", stream, n % NDMA_SLOTS)
        op.sigval = 16 * (n // NDMA_SLOTS + 1)
        op.signal = True
        lst = self.dma_ops.setdefault(stream, [])
        self._add(op, reads, writes)
        if n >= NDMA_SLOTS:
            op.deps.append(lst[n - NDMA_SLOTS])
        lst.append(op)
        return op

    def emit(self, stack):
        nc = self.nc
        sems = {}
        cnt = {}
        for op in self.ops:
            if (not op.is_dma) and op.signal:
                cnt[op.stream] = cnt.get(op.stream, 0) + 1
                op.semkey = ("eng", op.stream)
                op.sigval = cnt[op.stream]
        keys = set(op.semkey for op in self.ops if op.semkey is not None)
        for k in sorted(keys, key=str):
            sems[k] = stack.enter_context(nc.semaphore("s_" + "_".join(str(x) for x in k)))
        final = {}
        for op in self.ops:
            if op.signal and final.get(op.semkey, 0) < op.sigval:
                final[op.semkey] = op.sigval
        per = {s: [op for op in self.ops if op.stream == s] for s in self.streams}

        def body(stream):
            def f(eng):
                w = {}
                for op in per[stream]:
                    need = {}
                    for q in op.deps:
                        if need.get(q.semkey, 0) < q.sigval:
                            need[q.semkey] = q.sigval
                    for k, v in need.items():
                        if w.get(k, 0) < v:
                            eng.wait_ge(sems[k], v)
                            w[k] = v
                    inst = op.fn()
                    if op.signal:
                        inst.then_inc(sems[op.semkey], 16 if op.is_dma else 1)
                if stream == "sp":
                    for k, v in final.items():
                        eng.wait_ge(sems[k], v)
            return f

        with nc.Block() as block:
            block.tensor(body("pe"))
            block.scalar(body("act"))
            block.vector(body("dve"))
            block.gpsimd(body("pool"))
            block.sync(body("sp"))


def build(NSEQ, S, NL, stop_after=None):
    T = CTX + S
    NTT = T // 128
    NLT = S // 128
    NQC = S // 512
    NB = NSEQ + 1
    CH = [(0, 256)] + [(256 + 512 * i, 512) for i in range(NQC)]
    NCH = len(CH)
    FW = 544

    def chunk_of_tile(tt):
        return 0 if tt < 2 else 1 + (tt - 2) // 4

    nc = bass.Bass("TRN2", target_bir_lowering=False)

    def din(name, shape, dt=F32):
        return nc.dram_tensor(name, list(shape), dt, kind="ExternalInput").ap()

    x_d = din("x", [NSEQ, S, 1024])
    ctx_d = din("ctx", [NSEQ, CTX, 1024])
    W_d = din("W", [NL * NBLK_A + NL * NBLK_L, 128, 1024])
    cT_d = din("cT", [128, 8 * NB])
    bada_d = din("bada", [128, NL * 48])
    nrm_d = din("nrm", [128, NL * 2 * 8])
    qkg_d = din("qkg", [128, NL * 4])
    sink_d = din("sink", [128, NL * 6])
    psc_d = din("psc", [128, NL * 2])
    edge_d = din("edge", [128, 2 * 16])
    wpool_d = din("wpool", [128, NL * 2 * 128])
    rope_d = din("rope", [2, 128, S])
    ident_d = din("ident", [128, 128])
    mask_d = din("mask", [128, 6 * 512], BF16)
    cb_d = din("cb", [128, 3 * 128], BF16)
    out_d = nc.dram_tensor("out", [NSEQ, S, 1024], F32, kind="ExternalOutput").ap()
    xs_d = nc.dram_tensor("xs", [8, 128, T], F32, kind="Internal").ap()
    wbf_d = nc.dram_tensor("wbf", [NL * NBLK_L, 128, 1024], BF16, kind="Internal").ap()

    st = contextlib.ExitStack()
    with st:
        def sb(name, shape, dt):
            return st.enter_context(nc.sbuf_tensor(name, list(shape), dt))

        XTr = sb("XTr", [128, 8 * T], F32)
        HTt = sb("HT", [128, 8 * T], BF16)
        YTt = sb("YT", [128, 8 * T], BF16)
        NSTG, NWB, NPT, NF, NBT = 3, 6, 4, 7, 4
        STG = sb("STG", [128, NSTG * 1024], F32)
        WB = sb("WB", [128, NWB * 1024], BF16)
        PTt = sb("PT", [128, NPT * 512], BF16)
        MASK = sb("MASK", [128, 6 * 512], BF16)
        CB = sb("CB", [128, 3 * 128], BF16)
        IDENT = sb("IDENT", [128, 128], F32)
        FT = sb("FT", [128, NF * FW], F32)
        BT = sb("BTMP", [128, NBT * 512], BF16)
        MOD = sb("MOD", [128, NL * 48 * NB], F32)
        GS = sb("GS", [128, NL * 2 * 8 * NB], F32)
        NRM = sb("NRM", [128, NL * 16], F32)
        QKG = sb("QKG", [128, NL * 4], F32)
        ES = sb("ES", [128, NL * 6], F32)
        PSC = sb("PSC", [128, NL * 2], F32)
        EDGE = sb("EDGE", [128, 32], F32)
        BADA = sb("BADA", [128, NL * 48], F32)
        CT = sb("CT", [128, 8 * NB], F32)
        SCT = sb("SCT", [128, 8 * NB], BF16)
        WPF = sb("WPF", [128, NL * 256], F32)
        WPB = sb("WPB", [128, NL * 256], BF16)
        DUMMY = sb("DUMMY", [128, 8], F32)
        RSM = sb("RSM", [128, 512], F32)
        PSALL = st.enter_context(nc.psum_tensor("psall", [128, 8 * 512], F32))
        PS = [PSALL[:, i * 512:(i + 1) * 512] for i in range(8)]

        XT = XTr[:].rearrange("p (c t) -> p c t", c=8)
        HT = HTt[:].rearrange("p (c t) -> p c t", c=8)
        YT = YTt[:].rearrange("p (c t) -> p c t", c=8)
        XB = XTr[:].bitcast(BF16)
        OT = XB[:, 0:8 * T].rearrange("p (c t) -> p c t", c=8)
        KT = XB[:, 8 * T:10 * T].rearrange("p (c t) -> p c t", c=2)
        VA = XB[:, 10 * T:13 * T].rearrange("p (k c) -> p k c", c=384)
        QT = XB[:, 13 * T:15 * T].rearrange("p (c t) -> p c t", c=2)
        YF = YTt[:].bitcast(F32)
        ROPE_C = YF[:, 0:S]
        ROPE_S = YF[:, S:2 * S]
        UW = T + 32
        UT = YF[:, 2 * S:2 * S + 2 * UW].rearrange("p (c t) -> p c t", c=2)
        PERMb = CB[:, 0:128]
        BOb = CB[:, 128:256]
        ONESb = CB[:, 256:384]
        MODv = MOD[:].rearrange("p (l k c b) -> p l k c b", l=NL, k=6, c=8)
        GSv = GS[:].rearrange("p (l w c b) -> p l w c b", l=NL, w=2, c=8)
        NRMv = NRM[:].rearrange("p (l w c) -> p l w c", l=NL, w=2)

        p = Prog(nc)
        B = p.B
        UX = [B("UX")]
        UY = [B("UY")]

        def mm(out, lhsT, rhs, start, stop, reads, writes):
            p.op("pe", lambda: nc.tensor.matmul(out, lhsT=lhsT, rhs=rhs, start=start, stop=stop), reads, writes)

        def act(out, in_, func, reads, writes, scale=1.0, bias=0.0):
            p.op("act", lambda: nc.scalar.activation(out=out, in_=in_, func=func, bias=bias, scale=scale), reads, writes)

        def tt(eng, out, in0, in1, op, reads, writes):
            e = nc.vector if eng == "dve" else nc.gpsimd
            p.op(eng, lambda: e.tensor_tensor(out=out, in0=in0, in1=in1, op=op), reads, writes)

        def stt(out, in0, scalar, in1, op0, op1, reads, writes):
            p.op("dve", lambda: nc.vector.scalar_tensor_tensor(out=out, in0=in0, scalar=scalar, in1=in1,
                                                                op0=op0, op1=op1), reads, writes)

        def ts(eng, out, in0, s1, op0, reads, writes):
            e = nc.vector if eng == "dve" else nc.gpsimd
            p.op(eng, lambda: e.tensor_scalar(out=out, in0=in0, scalar1=s1, scalar2=None, op0=op0), reads, writes)

        def recip(out, in_, reads, writes):
            p.op("dve", lambda: nc.vector.reciprocal(out=out, in_=in_), reads, writes)

        def cp(eng, out, in_, reads, writes):
            e = {"dve": nc.vector, "pool": nc.gpsimd}[eng]
            p.op(eng, lambda: e.tensor_copy(out=out, in_=in_), reads, writes)

        def mset(eng, ap, val, reads, writes):
            e = {"dve": nc.vector, "pool": nc.gpsimd}[eng]
            p.op(eng, lambda: e.memset(ap, val), reads, writes)

        def dma(out, in_, reads, writes):
            p.dma("sp", lambda: nc.sync.dma_start(out=out, in_=in_), reads, writes)

        def join(reads, writes):
            p.op("pool", lambda: nc.gpsimd.memset(DUMMY[:, 0:1], 0.0), reads, list(writes) + [B("dummy")])

        rot = {}

        def ring(name, n):
            i = rot.get(name, 0)
            rot[name] = i + 1
            return i % n

        def ps(pool):
            banks = {"A": (0,), "B": (1,), "S": (2, 3, 4, 5), "O": (6, 7), "AO": (0, 1, 6, 7)}[pool]
            i = banks[ring("ps" + pool, len(banks))]
            return PS[i], B("ps", i)

        def ftmp():
            i = ring("ft", NF)
            return FT[:, i * FW:(i + 1) * FW], B("ft", i)

        def btmp():
            i = ring("bt", NBT)
            return BT[:, i * 512:(i + 1) * 512], B("bt", i)

        def ptmp():
            i = ring("pt", NPT)
            return PTt[:, i * 512:(i + 1) * 512], B("pt", i)

        order = [(i, 0) for i in range(NL * NBLK_A)]
        for sq in range(NSEQ):
            order += [(NL * NBLK_A + i, sq) for i in range(NL * NBLK_L)]
        wstate = {"load": 0, "use": 0, "slot": {}}

        def cached(i):
            blk, sq = order[i]
            return blk >= NL * NBLK_A and sq > 0 and not NOCACHE

        def w_prefetch(upto):
            while wstate["load"] < min(upto, len(order)):
                i = wstate["load"]
                blk, sq = order[i]
                if cached(i):
                    b = ring("wb", NWB)
                    wstate["slot"][i] = b
                    dma(WB[:, b * 1024:(b + 1) * 1024], wbf_d[blk - NL * NBLK_A], [B("wdram", blk)], [B("wb", b)])
                else:
                    s_ = ring("stg", NSTG)
                    wstate["slot"][i] = (s_, ring("wb", NWB))
                    dma(STG[:, s_ * 1024:(s_ + 1) * 1024], W_d[blk], [], [B("stg", s_)])
                wstate["load"] += 1

        def w_get():
            i = wstate["use"]
            w_prefetch(i + 3)
            blk, sq = order[i]
            if cached(i):
                b = wstate["slot"].pop(i)
            else:
                s_, b = wstate["slot"].pop(i)
                cp("dve", WB[:, b * 1024:(b + 1) * 1024], STG[:, s_ * 1024:(s_ + 1) * 1024], [B("stg", s_)], [B("wb", b)])
                if blk >= NL * NBLK_A and NSEQ > 1:
                    dma(wbf_d[blk - NL * NBLK_A], WB[:, b * 1024:(b + 1) * 1024], [B("wb", b)], [B("wdram", blk)])
            wstate["use"] += 1
            return WB[:, b * 1024:(b + 1) * 1024].rearrange("p (k c) -> p k c", k=8), B("wb", b)

        for dst, src, key in ((MASK, mask_d, "mask"), (CB, cb_d, "cb"), (IDENT, ident_d, "ident"), (NRM, nrm_d, "nrm"),
                              (QKG, qkg_d, "qkg"), (ES, sink_d, "es"), (PSC, psc_d, "psc"), (EDGE, edge_d, "edge"),
                              (BADA, bada_d, "bada"), (CT, cT_d, "ct"), (WPF, wpool_d, "wpf")):
            dma(dst[:], src, [], [B(key)])
        act(ES[:], ES[:], AF.Exp, [B("es")], [B("es")])
        cp("dve", WPB[:], WPF[:], [B("wpf")], [B("wpb")])
        act(SCT[:], CT[:], AF.Silu, [B("ct")], [B("sct")])
        SCTv = SCT[:].rearrange("p (k b) -> p k b", k=8)
        for l in range(NL):
            for m in range(48):
                wv, wb = w_get()
                P, Pb = ps("A")
                for kc in range(8):
                    mm(P[:, 0:NB], wv[:, kc, :], SCTv[:, kc, :], kc == 0, kc == 7, [wb, B("sct")], [Pb])
                ts("dve", MODv[:, l, m // 8, m % 8, :], P[:, 0:NB], BADA[:, l * 48 + m:l * 48 + m + 1], ALU.add,
                   [Pb, B("bada")], [B("mod")])
        for l in range(NL):
            for w in range(2):
                for col in range(NB):
                    stt(GSv[:, l, w, :, col], MODv[:, l, 1 + 3 * w, :, col], 1.0, NRMv[:, l, w, :], ALU.add, ALU.mult,
                        [B("mod"), B("nrm")], [B("gs")])

        def modvec(l, kind, c, col):
            return MODv[:, l, kind, c, col:col + 1]

        FULL = {"v": True}

        def CHE(full=None):
            f_ = FULL["v"] if full is None else full
            return [(ti, c) for ti, c in enumerate(CH) if f_ or ti > 0]

        def modulate(l, w, b):
            for ti, (t0, n) in CHE():
                col = NSEQ if ti == 0 else b
                P, Pb = ps("B")
                for kc in range(8):
                    sq, sqb = btmp()
                    act(sq[:, :n], XT[:, kc, t0:t0 + n], AF.Square, [B("xT", kc, ti)] + UX, [sqb])
                    mm(P[:, :n], ONESb, sq[:, :n], kc == 0, kc == 7, [sqb, B("cb")], [Pb])
                ln, lnb = ftmp()
                act(ln[:, :n], P[:, :n], AF.Ln, [Pb], [lnb], scale=1.0 / 1024, bias=EPS)
                rs, rsb = RSM, B("rsm")
                act(rs[:, :n], ln[:, :n], AF.Exp, [lnb], [rsb], scale=-0.5)
                for kc in range(8):
                    t, tb = ftmp()
                    tt("dve", t[:, :n], XT[:, kc, t0:t0 + n], rs[:, :n], ALU.mult, [B("xT", kc, ti), rsb] + UX, [tb])
                    act(HT[:, kc, t0:t0 + n], t[:, :n], AF.Identity, [tb, B("gs"), B("mod")], [B("hT", kc, ti)],
                        scale=GSv[:, l, w, kc, col:col + 1], bias=modvec(l, 3 * w, kc, col))

        def proj_acc(P, Pb, wv, wb, ti):
            t0, n = CH[ti]
            for kc in range(8):
                mm(P[:, :n], wv[:, kc, :], HT[:, kc, t0:t0 + n], kc == 0, kc == 7, [wb, B("hT", kc, ti)], [Pb])

        def qk_chunk(wv, wb, l, gidx, dest, dkey, full=None):
            g = QKG[:, l * 4 + gidx:l * 4 + gidx + 1]
            for ti, (t0, n) in CHE(full):
                A, Ab = ps("A")
                proj_acc(A, Ab, wv, wb, ti)
                sq, sqb = btmp()
                act(sq[:, :n], A[:, :n], AF.Square, [Ab], [sqb])
                yield
                Bp, Bb = ps("B")
                mm(Bp[:, :n], BOb, sq[:, :n], True, True, [sqb, B("cb")], [Bb])
                ln, lnb = ftmp()
                act(ln[:, :n], Bp[:, :n], AF.Ln, [Bb], [lnb], scale=1.0 / 64, bias=EPS)
                rs, rsb = ftmp()
                act(rs[:, :n], ln[:, :n], AF.Exp, [lnb], [rsb], scale=-0.5)
                if ti == 0:
                    stt(dest[:, t0:t0 + n], A[:, :n], g, rs[:, :n], ALU.mult, ALU.mult, [Ab, rsb, B("qkg")] + UX,
                        [dkey(ti)])
                    yield
                else:
                    l0 = t0 - CTX
                    qn, qnb = btmp()
                    stt(qn[:, :n], A[:, :n], g, rs[:, :n], ALU.mult, ALU.mult, [Ab, rsb, B("qkg")], [qnb])
                    yield
                    C, Cb = ps("B")
                    mm(C[:, :n], PERMb, qn[:, :n], True, True, [qnb, B("cb")], [Cb])
                    t1, t1b = ftmp()
                    tt("pool", t1[:, :n], qn[:, :n], ROPE_C[:, l0:l0 + n], ALU.mult, [qnb, B("rope")] + UY, [t1b])
                    t2, t2b = ftmp()
                    tt("dve", t2[:, :n], C[:, :n], ROPE_S[:, l0:l0 + n], ALU.mult, [Cb, B("rope")] + UY, [t2b])
                    tt("pool", dest[:, t0:t0 + n], t1[:, :n], t2[:, :n], ALU.add, [t1b, t2b] + UX, [dkey(ti)])
                    yield

        def drain(gen):
            for _ in gen:
                pass

        def interleave(main, side, ratio, which=0):
            if not ILV[which] and side is not None:
                drain(side)
                side = None
            for _ in main:
                for _k in range(ratio):
                    if side is not None and next(side, "end") == "end":
                        side = None
            if side is not None:
                drain(side)

        def v_block(wv, wb, which):
            base = which * 192
            for g0 in range(0, NTT, 4):
                ng = min(4, NTT - g0)
                P, Pb = ps("O")
                for j in range(ng):
                    tk = g0 + j
                    ti = chunk_of_tile(tk)
                    for kc in range(8):
                        mm(P[:, j * 128:(j + 1) * 128], HT[:, kc, tk * 128:(tk + 1) * 128], wv[:, kc, :], kc == 0, kc == 7,
                           [wb, B("hT", kc, ti)], [Pb])
                Pv = P[:, 0:ng * 128].rearrange("p (j c) -> p j c", c=128)
                vb = [B("va", g0 + j) for j in range(ng)]
                act(VA[:, g0:g0 + ng, base:base + 64], Pv[:, :, 0:64], AF.Copy, [Pb] + UX, vb)
                act(VA[:, g0:g0 + ng, base + 128:base + 192], Pv[:, :, 64:128], AF.Copy, [Pb] + UX, vb)
                yield

        def upos(t):
            return t + 8 if t < CTX else t + 24

        def u_block(wv, wb, uc):
            for ti, (t0, n) in CHE():
                A, Ab = ps("O")
                proj_acc(A, Ab, wv, wb, ti)
                a = upos(t0)
                act(UT[:, uc, a:a + n], A[:, :n], AF.Copy, [Ab] + UY, [B("ut", uc, ti)])
                yield

        def pooling(l, uc):
            wins = (2, 4) if uc == 0 else (8, 16)
            for ti, (t0, n) in CHE():
                a = upos(t0)
                ub = [B("ut", uc, k) for k in range(max(0, ti - 1), min(NCH, ti + 2))] + UY
                b2, b2b = ftmp()
                tt("pool", b2[:, 0:n + 14], UT[:, uc, a - 8:a + n + 6], UT[:, uc, a - 7:a + n + 7], ALU.add, ub, [b2b])
                b4, b4b = ftmp()
                tt("pool", b4[:, 0:n + 12], b2[:, 0:n + 12], b2[:, 2:n + 14], ALU.add, [b2b], [b4b])
                if uc == 0:
                    srcs = ((b2, b2b, 7), (b4, b4b, 6))
                else:
                    b8, b8b = ftmp()
                    tt("pool", b8[:, 0:n + 8], b4[:, 0:n + 8], b4[:, 4:n + 12], ALU.add, [b4b], [b8b])
                    b16, b16b = ftmp()
                    tt("pool", b16[:, 0:n], b8[:, 0:n], b8[:, 8:n + 8], ALU.add, [b8b], [b16b])
                    srcs = ((b8, b8b, 4), (b16, b16b, 0))
                pl, plb = btmp()
                for h in range(2):
                    rows = slice(h * 64, h * 64 + 64)
                    bx, bxb, off = srcs[h]
                    stt(pl[rows, :n], bx[rows, off:off + n], 1.0 / wins[h], UT[rows, uc, a:a + n], ALU.mult, ALU.subtract,
                        [bxb] + ub, [plb])
                edges = []
                if ti <= 1:
                    edges.append((0, 0))
                if ti == 0 or ti == NCH - 1:
                    edges.append((n - 8, 8))
                for (c0, e0) in edges:
                    e, eb = ftmp()
                    for h in range(2):
                        rows = slice(h * 64, h * 64 + 64)
                        bx, bxb, off = srcs[h]
                        tt("dve", e[rows, 0:8], bx[rows, off + c0:off + c0 + 8], EDGE[rows, uc * 16 + e0:uc * 16 + e0 + 8],
                           ALU.mult, [bxb, B("edge")], [eb])
                    tt("dve", pl[:, c0:c0 + 8], e[:, 0:8], UT[:, uc, a + c0:a + c0 + 8], ALU.subtract, [eb, plb] + ub, [plb])
                M, Mb = ps("A")
                mm(M[:, :n], WPB[:, (l * 2 + uc) * 128:(l * 2 + uc + 1) * 128], pl[:, :n], True, True, [plb, B("wpb")], [Mb])
                ts("dve", OT[:, 3 + uc, t0:t0 + n], M[:, :n], PSC[:, l * 2 + uc:l * 2 + uc + 1], ALU.mult,
                   [Mb, B("psc")] + UX, [B("oT", 3 + uc, ti)])

        def attention(l, typ, j, qbuf):
            orow = j if typ == 0 else 5 + j
            for qi, (q0, nq) in CHE():
                if qi == 0:
                    kts = [(0, None), (1, None)]
                elif typ == 0:
                    kts = [(kt, None) for kt in range(NTT)]
                else:
                    i0 = 4 * (qi - 1)
                    kts = [(0, None), (1, None)] + [(2 + jj, jj - i0) for jj in range(max(0, i0 - 1), min(NLT, i0 + 5))]
                nk = len(kts)

                def crange(d):
                    if d is None:
                        return 0, nq
                    return max(0, 128 * (d - 1)), min(nq, 128 * (d + 2))

                strm = []
                for hp in range(2):
                    O, Ob = ps("O")
                    strm.append(dict(hp=hp, prow=slice(hp * 64, hp * 64 + 64), drow=slice(64 - hp * 64, 128 - hp * 64),
                                     head=j + 3 * hp, vbase=typ * 192 + hp * 64, O=O, Ob=Ob, pend=[]))

                def pv(sm, idx):
                    kt, pt, ptb, c0, c1 = sm["pend"][idx]
                    mm(sm["O"][:, c0:c1], VA[:, kt, sm["vbase"]:sm["vbase"] + 128], pt[:, c0:c1], idx == 0, idx == nk - 1,
                       [ptb, B("va", kt)] + UX, [sm["Ob"]])

                for idx, (kt, d) in enumerate(kts):
                    c0, c1 = crange(d)
                    pr = ring("spair", 2)
                    banks = (2 + 2 * pr, 3 + 2 * pr)
                    sbufs = [B("ps", banks[0]), B("ps", banks[1])]
                    qr = ring("ptpair", NPT // 2)
                    ptbufs = [B("pt", 2 * qr), B("pt", 2 * qr + 1)]
                    for sm in strm:
                        prow = sm["prow"]
                        Sp = PS[banks[sm["hp"]]]
                        mm(Sp[:, c0:c1], KT[prow, typ, kt * 128:(kt + 1) * 128], QT[prow, qbuf, q0 + c0:q0 + c1], True, True,
                           [B("kT", typ, chunk_of_tile(kt)), B("qT", qbuf, qi)] + UX, [sbufs[sm["hp"]]])
                    Sw = PSALL[:, banks[0] * 512:(banks[1] + 1) * 512].rearrange("p (s n) -> p s n", s=2)
                    Pw = PTt[:, 2 * qr * 512:(2 * qr + 2) * 512].rearrange("p (s n) -> p s n", s=2)
                    act(Pw[:, :, c0:c1], Sw[:, :, c0:c1], AF.Exp, sbufs, ptbufs, scale=0.125)
                    for sm in strm:
                        pt = PTt[:, (2 * qr + sm["hp"]) * 512:(2 * qr + sm["hp"] + 1) * 512]
                        ptb = ptbufs[sm["hp"]]
                        if d is not None:
                            tt("pool", pt[:, c0:c1], pt[:, c0:c1],
                               MASK[:, (d + 1) * 512 + c0:(d + 1) * 512 + c1], ALU.mult, [ptb, B("mask")], [ptb])
                        sm["pend"].append((kt, pt, ptb, c0, c1))
                    if idx >= 1:
                        for sm in strm:
                            pv(sm, idx - 1)
                    if idx % 6 == 5 and idx < nk - 1:
                        yield
                for sm in strm:
                    pv(sm, nk - 1)
                yield
                evs = []
                for sm in strm:
                    ob, obb = ftmp()
                    cp("dve", ob[:, :nq], sm["O"][:, :nq], [sm["Ob"]], [obb])
                    evs.append((ob, obb))
                for sm, (ob, obb) in zip(strm, evs):
                    prow, drow = sm["prow"], sm["drow"]
                    rec, recb = ftmp()
                    if typ == 1:
                        ts("dve", rec[prow, :nq], ob[drow, :nq], ES[prow, l * 6 + sm["head"]:l * 6 + sm["head"] + 1], ALU.add,
                           [obb, B("es")], [recb])
                        recip(rec[prow, :nq], rec[prow, :nq], [recb], [recb])
                    else:
                        recip(rec[prow, :nq], ob[drow, :nq], [obb], [recb])
                    tt("dve", OT[prow, orow, q0:q0 + nq], ob[prow, :nq], rec[prow, :nq], ALU.mult, [obb, recb] + UX,
                       [B("oT", orow, qi)])

        def merge(l):
            for c in range(8):
                wg = [w_get() for _ in range(3)]
                wbr = w_get()
                for ti, (t0, n) in CHE():
                    ms = []
                    for br, (r0, nr) in enumerate(((0, 3), (3, 2), (5, 3))):
                        G, Gb = ps("S")
                        proj_acc(G, Gb, wg[br][0], wg[br][1], ti)
                        sg, sgb = ftmp()
                        act(sg[:, :n], G[:, :n], AF.Sigmoid, [Gb], [sgb])
                        Bp, Bb = ps("AO")
                        for r in range(nr):
                            mm(Bp[:, :n], wbr[0][:, r0 + r, :], OT[:, r0 + r, t0:t0 + n], r == 0, r == nr - 1,
                               [wbr[1], B("oT", r0 + r, ti)] + UX, [Bb])
                        m, mb = ftmp()
                        tt("dve", m[:, :n], Bp[:, :n], sg[:, :n], ALU.mult, [Bb, sgb], [mb])
                        ms.append((m, mb))
                    s, sbf = ftmp()
                    tt("pool", s[:, :n], ms[0][0][:, :n], ms[1][0][:, :n], ALU.add, [ms[0][1], ms[1][1]], [sbf])
                    tt("pool", YT[:, c, t0:t0 + n], s[:, :n], ms[2][0][:, :n], ALU.add, [sbf, ms[2][1]] + UY,
                       [B("yT", c, ti)])

        def wout(l, b):
            join([], UX)
            for c in range(8):
                dma(XT[:, c, :], xs_d[c], UX, [B("xT", c, ti) for ti in range(NCH)])
            for c in range(8):
                wv, wb = w_get()
                for ti, (t0, n) in CHE():
                    col = NSEQ if ti == 0 else b
                    P, Pb = ps("AO")
                    for kc in range(8):
                        mm(P[:, :n], wv[:, kc, :], YT[:, kc, t0:t0 + n], kc == 0, kc == 7, [wb, B("yT", kc, ti)] + UY, [Pb])
                    stt(XT[:, c, t0:t0 + n], P[:, :n], modvec(l, 2, c, col), XT[:, c, t0:t0 + n], ALU.mult, ALU.add,
                        [Pb, B("mod"), B("xT", c, ti)] + UX, [B("xT", c, ti)])

        def mlp(l, b):
            for g in range(4):
                for kh in range(8):
                    wv, wb = w_get()
                    for ti, (t0, n) in CHE():
                        P, Pb = ps("S")
                        proj_acc(P, Pb, wv, wb, ti)
                        r, rb = ftmp()
                        act(r[:, :n], P[:, :n], AF.Relu, [Pb], [rb])
                        tt("dve", YT[:, kh, t0:t0 + n], r[:, :n], r[:, :n], ALU.mult, [rb] + UY, [B("yT", kh, ti)])
                for c in range(8):
                    wv, wb = w_get()
                    for ti, (t0, n) in CHE():
                        col = NSEQ if ti == 0 else b
                        P, Pb = ps("AO")
                        for kh in range(8):
                            mm(P[:, :n], wv[:, kh, :], YT[:, kh, t0:t0 + n], kh == 0, kh == 7, [wb, B("yT", kh, ti)] + UY, [Pb])
                        stt(XT[:, c, t0:t0 + n], P[:, :n], modvec(l, 5, c, col), XT[:, c, t0:t0 + n], ALU.mult, ALU.add,
                            [Pb, B("mod"), B("xT", c, ti)] + UX, [B("xT", c, ti)])

        def load_seq(b):
            for tk in range(NTT):
                ti = chunk_of_tile(tk)
                for half in range(2):
                    f, fb = ftmp()
                    cs = slice(half * 512, (half + 1) * 512)
                    src = ctx_d[b, tk * 128:(tk + 1) * 128, cs] if tk < 2 else x_d[b, (tk - 2) * 128:(tk - 1) * 128, cs]
                    dma(f[:, 0:512], src, [], [fb])
                    P, Pb = ps("AO")
                    for q in range(4):
                        p.op("pe", (lambda o=P[:, q * 128:(q + 1) * 128], i=f[:, q * 128:(q + 1) * 128]:
                                    nc.tensor.transpose(o, i, IDENT[:])), [fb, B("ident")], [Pb])
                    Pv = P[:].rearrange("p (j c) -> p j c", c=128)
                    cp("dve", XT[:, half * 4:half * 4 + 4, tk * 128:(tk + 1) * 128], Pv, [Pb] + UX,
                       [B("xT", half * 4 + q, ti) for q in range(4)])

        def store_seq(b):
            for lt in range(NLT):
                tk = 2 + lt
                ti = chunk_of_tile(tk)
                for half in range(2):
                    P, Pb = ps("AO")
                    for q in range(4):
                        kc = half * 4 + q
                        p.op("pe", (lambda o=P[:, q * 128:(q + 1) * 128], i=XT[:, kc, tk * 128:(tk + 1) * 128]:
                                    nc.tensor.transpose(o, i, IDENT[:])), [B("xT", kc, ti), B("ident")] + UX, [Pb])
                    f, fb = ftmp()
                    cp("dve", f[:, 0:512], P[:], [Pb], [fb])
                    dma(out_d[b, lt * 128:(lt + 1) * 128, half * 512:(half + 1) * 512], f[:, 0:512], [fb], [])

        for b in range(NSEQ):
            load_seq(b)
            for l in range(NL):
                FULL["v"] = True
                modulate(l, 0, b)
                for c in range(8):
                    dma(xs_d[c], XT[:, c, :], [B("xT", c, ti) for ti in range(NCH)] + UX, [B("spill", c)])
                join([B("spill", c) for c in range(8)] + [B("hT", kc, ti) for kc in range(8) for ti in range(NCH)], UX)
                join([], UY)
                dma(ROPE_C, rope_d[0], UY, [B("rope")])
                dma(ROPE_S, rope_d[1], UY + [B("rope")], [B("rope")])
                for uc in range(2):
                    mset("pool", UT[:, uc, :], 0.0, UY, [B("ut", uc, ti) for ti in range(NCH)])
                for tk in range(NTT):
                    mset("pool", VA[:, tk, 64:128], 1.0, UX, [B("va", tk)])
                    mset("pool", VA[:, tk, 256:320], 1.0, UX, [B("va", tk)])
                def both(*gens):
                    for g_ in gens:
                        yield from g_

                wk, wkb = w_get()
                wva, wvab = w_get()
                wvc, wvcb = w_get()
                interleave(both(v_block(wva, wvab, 0), v_block(wvc, wvcb, 1)),
                           qk_chunk(wk, wkb, l, 1, KT[:, 0, :], lambda ti: B("kT", 0, ti), full=True), 2)
                wk, wkb = w_get()
                wu0, wu0b = w_get()
                wu1, wu1b = w_get()
                FULL["v"] = l < NL - 1
                interleave(both(u_block(wu0, wu0b, 0), u_block(wu1, wu1b, 1)),
                           qk_chunk(wk, wkb, l, 3, KT[:, 1, :], lambda ti: B("kT", 1, ti), full=True), 2)
                for uc in range(2):
                    pooling(l, uc)
                order_q = [(typ, j) for typ in range(2) for j in range(3)]
                wv, wb = w_get()
                drain(qk_chunk(wv, wb, l, 0, QT[:, 0, :], lambda ti: B("qT", 0, ti)))
                for idx, (typ, j) in enumerate(order_q):
                    qb = idx % 2
                    side = None
                    if idx + 1 < len(order_q):
                        wv, wb = w_get()
                        nqb = (idx + 1) % 2
                        side = qk_chunk(wv, wb, l, 2 * order_q[idx + 1][0], QT[:, nqb, :], lambda ti, nqb=nqb: B("qT", nqb, ti))
                    interleave(attention(l, typ, j, qb), side, 1, 1)
                if stop_after == "attn":
                    break
                join([], UY)
                merge(l)
                if stop_after == "merge":
                    break
                wout(l, b)
                if stop_after == "wout":
                    break
                modulate(l, 1, b)
                mlp(l, b)
                if stop_after == "layer0":
                    break
            if stop_after is None:
                store_seq(b)

        p.emit(st)
    return nc


def _blk(w, rows, cols):
    sub = w[np.ix_(rows, cols)] if not isinstance(rows, slice) else w[rows][:, cols]
    return np.ascontiguousarray(sub.reshape(8, 128, 128).transpose(1, 0, 2)).reshape(128, 1024)


def _layer_blocks(l, w_in, w_br_a, w_br_b, w_br_c, w_out, w_mlp1, w_mlp2):
    ar = np.arange
    allr = slice(0, 1024)
    wi = w_in[l]
    qa0, ka0, va0, qc0, kc0, vc0, u0, ga0 = 0, 384, 512, 640, 1024, 1152, 1280, 1536
    blocks = []

    def qcols(base, j):
        return np.concatenate([base + j * 64 + ar(64), base + (j + 3) * 64 + ar(64)])

    blocks.append(_blk(wi, allr, ka0 + ar(128)))
    blocks.append(_blk(wi, allr, va0 + ar(128)))
    blocks.append(_blk(wi, allr, vc0 + ar(128)))
    blocks.append(_blk(wi, allr, kc0 + ar(128)))
    blocks.append(_blk(wi, allr, u0 + ar(128)))
    blocks.append(_blk(wi, allr, u0 + 128 + ar(128)))
    for j in range(3):
        blocks.append(_blk(wi, allr, qcols(qa0, j)))
    for j in range(3):
        blocks.append(_blk(wi, allr, qcols(qc0, j)))
    rows_a = np.concatenate([qcols(0, j) for j in range(3)])
    wbr = np.concatenate([w_br_a[l][rows_a], w_br_b[l], w_br_c[l][rows_a]], axis=0)
    for c in range(8):
        for br in range(3):
            blocks.append(_blk(wi, allr, ga0 + br * 1024 + c * 128 + ar(128)))
        blocks.append(_blk(wbr, allr, c * 128 + ar(128)))
    for c in range(8):
        blocks.append(_blk(w_out[l], allr, c * 128 + ar(128)))
    for g in range(4):
        for kh in range(8):
            blocks.append(_blk(w_mlp1[l], allr, (g * 8 + kh) * 128 + ar(128)))
        for c in range(8):
            blocks.append(_blk(w_mlp2[l], slice(g * 1024, (g + 1) * 1024), c * 128 + ar(128)))
    assert len(blocks) == NBLK_L
    return blocks


def _consts(S):
    ar = np.arange
    t = ar(S, dtype=np.float32)
    r = np.floor(t / 64.0).astype(np.float32)
    col = (t - r * 64.0).astype(np.float32)
    inv = (1.0 / (10000.0 ** (ar(0, 32, 2, dtype=np.float32) / 32.0))).astype(np.float32)
    ang = np.concatenate([r[:, None] * inv, col[:, None] * inv], axis=-1).astype(np.float32)
    cos, sin = np.cos(ang).astype(np.float32), np.sin(ang).astype(np.float32)
    d = ar(128) % 64
    C = cos[:, d // 2].T.copy()
    Sg = (sin[:, d // 2] * np.where(d % 2 == 0, -1.0, 1.0)[None, :]).T.copy()
    rope = np.stack([C, Sg]).astype(np.float32)
    k = ar(128)[:, None]
    q = ar(512)[None, :]
    mask = np.concatenate([(np.abs(q - k - 128 * dd) <= 128).astype(np.float32) for dd in range(-1, 5)], axis=1)
    perm = np.zeros((128, 128), np.float32)
    perm[ar(128) ^ 1, ar(128)] = 1.0
    bo = (ar(128)[:, None] // 64 == ar(128)[None, :] // 64).astype(np.float32)
    ones = np.ones((128, 128), np.float32)
    cb = np.concatenate([perm, bo, ones], axis=1)
    ident = np.eye(128, dtype=np.float32)
    edge = np.zeros((128, 2, 16), np.float32)
    for uc in range(2):
        for h in range(2):
            w = (2, 4, 8, 16)[uc * 2 + h]
            for i in range(8):
                edge[h * 64:(h + 1) * 64, uc, i] = 1.0 / min(w, i + w // 2)
                edge[h * 64:(h + 1) * 64, uc, 8 + i] = 1.0 / min(w, 8 - i + w // 2)
    return dict(rope=rope, mask=mask.astype(ml_dtypes.bfloat16), cb=cb.astype(ml_dtypes.bfloat16), ident=ident,
                edge=edge.reshape(128, 32))


def prep_shared(NL, S, w_ada, b_ada, norm1, norm2, w_in, q_norm_a, k_norm_a, q_norm_c, k_norm_c, sink_c, w_pool,
                pool_scale, w_br_a, w_br_b, w_br_c, w_out, w_mlp1, w_mlp2):
    ar = np.arange
    f = lambda a: np.asarray(a, dtype=np.float32)
    w_ada, b_ada, w_in, w_out, w_mlp1, w_mlp2 = f(w_ada), f(b_ada), f(w_in), f(w_out), f(w_mlp1), f(w_mlp2)
    w_br_a, w_br_b, w_br_c = f(w_br_a), f(w_br_b), f(w_br_c)
    blocks = []
    for l in range(NL):
        for m in range(48):
            blocks.append(_blk(w_ada[l], slice(0, 1024), m * 128 + ar(128)))
    for l in range(NL):
        blocks += _layer_blocks(l, w_in, w_br_a, w_br_b, w_br_c, w_out, w_mlp1, w_mlp2)
    W = np.stack(blocks)
    sh = dict(W=W)
    sh["bada"] = np.ascontiguousarray(b_ada[:NL].reshape(NL, 48, 128).transpose(2, 0, 1)).reshape(128, NL * 48)
    nrm = np.stack([f(norm1)[:NL], f(norm2)[:NL]], axis=1)
    sh["nrm"] = np.ascontiguousarray(nrm.reshape(NL, 2, 8, 128).transpose(3, 0, 1, 2)).reshape(128, NL * 16)
    g4 = np.stack([f(q_norm_a)[:NL], f(k_norm_a)[:NL], f(q_norm_c)[:NL], f(k_norm_c)[:NL]], axis=1)
    sh["qkg"] = np.ascontiguousarray(np.concatenate([g4, g4], axis=2).transpose(2, 0, 1)).reshape(128, NL * 4)
    sh["sink"] = np.ascontiguousarray(np.broadcast_to(f(sink_c)[:NL].reshape(1, NL * 6), (128, NL * 6)))
    sh["psc"] = np.ascontiguousarray(f(pool_scale)[:NL].reshape(NL, 2, 128).transpose(2, 0, 1)).reshape(128, NL * 2)
    wp = np.zeros((128, NL, 2, 128), np.float32)
    wpl = f(w_pool)
    for l in range(NL):
        for uc in range(2):
            for h in range(2):
                wp[h * 64:(h + 1) * 64, l, uc, h * 64:(h + 1) * 64] = wpl[l, uc * 2 + h]
    sh["wpool"] = wp.reshape(128, NL * 256)
    sh.update(_consts(S))
    return sh


def prep_core(c_rows, c_ctx):
    cc = np.concatenate([np.asarray(c_rows, np.float32), np.asarray(c_ctx, np.float32)[None]], axis=0)
    NBc = cc.shape[0]
    return np.ascontiguousarray(cc.reshape(NBc, 8, 128).transpose(2, 1, 0)).reshape(128, 8 * NBc)


_NC_CACHE = {}


def kernel(x, c, ctx, c_ctx, w_ada, b_ada, norm1, norm2, w_in, q_norm_a, k_norm_a, q_norm_c, k_norm_c,
           sink_c, w_pool, pool_scale, w_br_a, w_br_b, w_br_c, w_out, w_mlp1, w_mlp2):
    x = np.asarray(x, np.float32)
    c = np.asarray(c, np.float32)
    ctx = np.asarray(ctx, np.float32)
    NCORES = 8
    Bt, S, _ = x.shape
    NL = np.asarray(w_in).shape[0]
    NSEQ = Bt // NCORES
    key = (NSEQ, S, NL)
    if key not in _NC_CACHE:
        _NC_CACHE[key] = build(NSEQ, S, NL)
    nc = _NC_CACHE[key]
    sh = prep_shared(NL, S, w_ada, b_ada, norm1, norm2, w_in, q_norm_a, k_norm_a, q_norm_c, k_norm_c, sink_c, w_pool,
                     pool_scale, w_br_a, w_br_b, w_br_c, w_out, w_mlp1, w_mlp2)
    in_maps = []
    for i in range(NCORES):
        sl = slice(i * NSEQ, (i + 1) * NSEQ)
        m = dict(sh)
        m["x"] = np.ascontiguousarray(x[sl])
        m["ctx"] = np.ascontiguousarray(ctx[sl])
        m["cT"] = prep_core(c[sl], c_ctx)
        in_maps.append(m)
    res = run_bass_kernel_spmd(nc, in_maps, core_ids=list(range(NCORES)))
    return np.concatenate([r["out"] for r in res.results], axis=0).astype(np.float32)
```

```python
import contextlib
import numpy as np
import ml_dtypes
import concourse.bass as bass
import concourse.mybir as mybir
from concourse.bass_utils import run_bass_kernel_spmd

F32 = mybir.dt.float32
BF16 = mybir.dt.bfloat16
AF = mybir.ActivationFunctionType
ALU = mybir.AluOpType

NDMA_SLOTS = 8
CTX = 256
EPS = 1e-6
NBLK_L = 116
NBLK_A = 48
NOCACHE = False
ILV = [True, True]


class Buf:
    __slots__ = ("name", "lw", "rd")

    def __init__(self, name):
        self.name = name
        self.lw = None
        self.rd = {}


class Op:
    __slots__ = ("stream", "fn", "deps", "signal", "is_dma", "semkey", "sigval")

    def __init__(self, stream, fn, is_dma):
        self.stream = stream
        self.fn = fn
        self.deps = []
        self.signal = False
        self.is_dma = is_dma
        self.semkey = None
        self.sigval = None


class Prog:
    def __init__(self, nc):
        self.nc = nc
        self.ops = []
        self.bufs = {}
        self.dma_count = {}
        self.dma_ops = {}
        self.streams = ("pe", "act", "dve", "pool", "sp")

    def B(self, *key):
        b = self.bufs.get(key)
        if b is None:
            b = Buf(key)
            self.bufs[key] = b
        return b

    def _add(self, op, reads, writes):
        deps = {}

        def need(p, raw):
            if p is None:
                return
            if (not p.is_dma) and (not op.is_dma) and p.stream == op.stream:
                if op.stream == "pe":
                    return
            deps[id(p)] = p

        for r in reads:
            need(r.lw, True)
        for w in writes:
            need(w.lw, False)
            for q in w.rd.values():
                need(q, False)
        op.deps = list(deps.values())
        for q in op.deps:
            q.signal = True
        for r in reads:
            r.rd[id(op) if op.is_dma else op.stream] = op
        for w in writes:
            w.lw = op
            w.rd = {}
        self.ops.append(op)
        return op

    def op(self, stream, fn, reads=(), writes=()):
        return self._add(Op(stream, fn, False), reads, writes)

    def dma(self, stream, fn, reads=(), writes=()):
        op = Op(stream, fn, True)
        n = self.dma_count.get(stream, 0)
        self.dma_count[stream] = n + 1
        op.semkey = ("dma", stream, n % NDMA_SLOTS)
        op.sigval = 16 * (n // NDMA_SLOTS + 1)
        op.signal = True
        lst = self.dma_ops.setdefault(stream, [])
        self._add(op, reads, writes)
        if n >= NDMA_SLOTS:
            op.deps.append(lst[n - NDMA_SLOTS])
        lst.append(op)
        return op

    def emit(self, stack):
        nc = self.nc
        sems = {}
        cnt = {}
        for op in self.ops:
            if (not op.is_dma) and op.signal:
                cnt[op.stream] = cnt.get(op.stream, 0) + 1
                op.semkey = ("eng", op.stream)
                op.sigval = cnt[op.stream]
        keys = set(op.semkey for op in self.ops if op.semkey is not None)
        for k in sorted(keys, key=str):
            sems[k] = stack.enter_context(nc.semaphore("s_" + "_".join(str(x) for x in k)))
        final = {}
        for op in self.ops:
            if op.signal and final.get(op.semkey, 0) < op.sigval:
                final[op.semkey] = op.sigval
        per = {s: [op for op in self.ops if op.stream == s] for s in self.streams}

        def body(stream):
            def f(eng):
                w = {}
                for op in per[stream]:
                    need = {}
                    for q in op.deps:
                        if need.get(q.semkey, 0) < q.sigval:
                            need[q.semkey] = q.sigval
                    for k, v in need.items():
                        if w.get(k, 0) < v:
                            eng.wait_ge(sems[k], v)
                            w[k] = v
                    inst = op.fn()
                    if op.signal:
                        inst.then_inc(sems[op.semkey], 16 if op.is_dma else 1)
                if stream == "sp":
                    for k, v in final.items():
                        eng.wait_ge(sems[k], v)
            return f

        with nc.Block() as block:
            block.tensor(body("pe"))
            block.scalar(body("act"))
            block.vector(body("dve"))
            block.gpsimd(body("pool"))
            block.sync(body("sp"))


def build(NSEQ, S, NL, stop_after=None):
    T = CTX + S
    NTT = T // 128
    NLT = S // 128
    NQC = S // 512
    NB = NSEQ + 1
    CH = [(0, 256)] + [(256 + 512 * i, 512) for i in range(NQC)]
    NCH = len(CH)
    FW = 544

    def chunk_of_tile(tt):
        return 0 if tt < 2 else 1 + (tt - 2) // 4

    nc = bass.Bass("TRN2", target_bir_lowering=False)

    def din(name, shape, dt=F32):
        return nc.dram_tensor(name, list(shape), dt, kind="ExternalInput").ap()

    x_d = din("x", [NSEQ, S, 1024])
    ctx_d = din("ctx", [NSEQ, CTX, 1024])
    W_d = din("W", [NL * NBLK_A + NL * NBLK_L, 128, 1024])
    cT_d = din("cT", [128, 8 * NB])
    bada_d = din("bada", [128, NL * 48])
    nrm_d = din("nrm", [128, NL * 2 * 8])
    qkg_d = din("qkg", [128, NL * 4])
    sink_d = din("sink", [128, NL * 6])
    psc_d = din("psc", [128, NL * 2])
    edge_d = din("edge", [128, 2 * 16])
    wpool_d = din("wpool", [128, NL * 2 * 128])
    rope_d = din("rope", [2, 128, S])
    ident_d = din("ident", [128, 128])
    mask_d = din("mask", [128, 6 * 512], BF16)
    cb_d = din("cb", [128, 3 * 128], BF16)
    out_d = nc.dram_tensor("out", [NSEQ, S, 1024], F32, kind="ExternalOutput").ap()
    xs_d = nc.dram_tensor("xs", [8, 128, T], F32, kind="Internal").ap()
    wbf_d = nc.dram_tensor("wbf", [NL * NBLK_L, 128, 1024], BF16, kind="Internal").ap()

    st = contextlib.ExitStack()
    with st:
        def sb(name, shape, dt):
            return st.enter_context(nc.sbuf_tensor(name, list(shape), dt))

        XTr = sb("XTr", [128, 8 * T], F32)
        HTt = sb("HT", [128, 8 * T], BF16)
        YTt = sb("YT", [128, 8 * T], BF16)
        NSTG, NWB, NPT, NF, NBT = 3, 6, 4, 7, 4
        STG = sb("STG", [128, NSTG * 1024], F32)
        WB = sb("WB", [128, NWB * 1024], BF16)
        PTt = sb("PT", [128, NPT * 512], BF16)
        MASK = sb("MASK", [128, 6 * 512], BF16)
        CB = sb("CB", [128, 3 * 128], BF16)
        IDENT = sb("IDENT", [128, 128], F32)
        FT = sb("FT", [128, NF * FW], F32)
        BT = sb("BTMP", [128, NBT * 512], BF16)
        MOD = sb("MOD", [128, NL * 48 * NB], F32)
        GS = sb("GS", [128, NL * 2 * 8 * NB], F32)
        NRM = sb("NRM", [128, NL * 16], F32)
        QKG = sb("QKG", [128, NL * 4], F32)
        ES = sb("ES", [128, NL * 6], F32)
        PSC = sb("PSC", [128, NL * 2], F32)
        EDGE = sb("EDGE", [128, 32], F32)
        BADA = sb("BADA", [128, NL * 48], F32)
        CT = sb("CT", [128, 8 * NB], F32)
        SCT = sb("SCT", [128, 8 * NB], BF16)
        WPF = sb("WPF", [128, NL * 256], F32)
        WPB = sb("WPB", [128, NL * 256], BF16)
        DUMMY = sb("DUMMY", [128, 8], F32)
        RSM = sb("RSM", [128, 512], F32)
        PS = [st.enter_context(nc.psum_tensor(f"ps{i}", [128, 512], F32)) for i in range(8)]

        XT = XTr[:].rearrange("p (c t) -> p c t", c=8)
        HT = HTt[:].rearrange("p (c t) -> p c t", c=8)
        YT = YTt[:].rearrange("p (c t) -> p c t", c=8)
        XB = XTr[:].bitcast(BF16)
        OT = XB[:, 0:8 * T].rearrange("p (c t) -> p c t", c=8)
        KT = XB[:, 8 * T:10 * T].rearrange("p (c t) -> p c t", c=2)
        VA = XB[:, 10 * T:13 * T].rearrange("p (k c) -> p k c", c=384)
        QT = XB[:, 13 * T:15 * T].rearrange("p (c t) -> p c t", c=2)
        YF = YTt[:].bitcast(F32)
        ROPE_C = YF[:, 0:S]
        ROPE_S = YF[:, S:2 * S]
        UW = T + 32
        UT = YF[:, 2 * S:2 * S + 2 * UW].rearrange("p (c t) -> p c t", c=2)
        PERMb = CB[:, 0:128]
        BOb = CB[:, 128:256]
        ONESb = CB[:, 256:384]
        MODv = MOD[:].rearrange("p (l k c b) -> p l k c b", l=NL, k=6, c=8)
        GSv = GS[:].rearrange("p (l w c b) -> p l w c b", l=NL, w=2, c=8)
        NRMv = NRM[:].rearrange("p (l w c) -> p l w c", l=NL, w=2)

        p = Prog(nc)
        B = p.B
        UX = [B("UX")]
        UY = [B("UY")]

        def mm(out, lhsT, rhs, start, stop, reads, writes):
            p.op("pe", lambda: nc.tensor.matmul(out, lhsT=lhsT, rhs=rhs, start=start, stop=stop), reads, writes)

        def act(out, in_, func, reads, writes, scale=1.0, bias=0.0):
            p.op("act", lambda: nc.scalar.activation(out=out, in_=in_, func=func, bias=bias, scale=scale), reads, writes)

        def tt(eng, out, in0, in1, op, reads, writes):
            e = nc.vector if eng == "dve" else nc.gpsimd
            p.op(eng, lambda: e.tensor_tensor(out=out, in0=in0, in1=in1, op=op), reads, writes)

        def stt(out, in0, scalar, in1, op0, op1, reads, writes):
            p.op("dve", lambda: nc.vector.scalar_tensor_tensor(out=out, in0=in0, scalar=scalar, in1=in1,
                                                                op0=op0, op1=op1), reads, writes)

        def ts(eng, out, in0, s1, op0, reads, writes):
            e = nc.vector if eng == "dve" else nc.gpsimd
            p.op(eng, lambda: e.tensor_scalar(out=out, in0=in0, scalar1=s1, scalar2=None, op0=op0), reads, writes)

        def recip(out, in_, reads, writes):
            p.op("dve", lambda: nc.vector.reciprocal(out=out, in_=in_), reads, writes)

        def cp(eng, out, in_, reads, writes):
            e = {"dve": nc.vector, "pool": nc.gpsimd}[eng]
            p.op(eng, lambda: e.tensor_copy(out=out, in_=in_), reads, writes)

        def mset(eng, ap, val, reads, writes):
            e = {"dve": nc.vector, "pool": nc.gpsimd}[eng]
            p.op(eng, lambda: e.memset(ap, val), reads, writes)

        def dma(out, in_, reads, writes):
            p.dma("sp", lambda: nc.sync.dma_start(out=out, in_=in_), reads, writes)

        def join(reads, writes):
            p.op("pool", lambda: nc.gpsimd.memset(DUMMY[:, 0:1], 0.0), reads, list(writes) + [B("dummy")])

        rot = {}

        def ring(name, n):
            i = rot.get(name, 0)
            rot[name] = i + 1
            return i % n

        def ps(pool):
            banks = {"A": (0,), "B": (1,), "S": (2, 3, 4, 5), "O": (6, 7), "AO": (0, 1, 6, 7)}[pool]
            i = banks[ring("ps" + pool, len(banks))]
            return PS[i], B("ps", i)

        def ftmp():
            i = ring("ft", NF)
            return FT[:, i * FW:(i + 1) * FW], B("ft", i)

        def btmp():
            i = ring("bt", NBT)
            return BT[:, i * 512:(i + 1) * 512], B("bt", i)

        def ptmp():
            i = ring("pt", NPT)
            return PTt[:, i * 512:(i + 1) * 512], B("pt", i)

        order = [(i, 0) for i in range(NL * NBLK_A)]
        for sq in range(NSEQ):
            order += [(NL * NBLK_A + i, sq) for i in range(NL * NBLK_L)]
        wstate = {"load": 0, "use": 0, "slot": {}}

        def cached(i):
            blk, sq = order[i]
            return blk >= NL * NBLK_A and sq > 0 and not NOCACHE

        def w_prefetch(upto):
            while wstate["load"] < min(upto, len(order)):
                i = wstate["load"]
                blk, sq = order[i]
                if cached(i):
                    b = ring("wb", NWB)
                    wstate["slot"][i] = b
                    dma(WB[:, b * 1024:(b + 1) * 1024], wbf_d[blk - NL * NBLK_A], [B("wdram", blk)], [B("wb", b)])
                else:
                    s_ = ring("stg", NSTG)
                    wstate["slot"][i] = (s_, ring("wb", NWB))
                    dma(STG[:, s_ * 1024:(s_ + 1) * 1024], W_d[blk], [], [B("stg", s_)])
                wstate["load"] += 1

        def w_get():
            i = wstate["use"]
            w_prefetch(i + 3)
            blk, sq = order[i]
            if cached(i):
                b = wstate["slot"].pop(i)
            else:
                s_, b = wstate["slot"].pop(i)
                cp("dve", WB[:, b * 1024:(b + 1) * 1024], STG[:, s_ * 1024:(s_ + 1) * 1024], [B("stg", s_)], [B("wb", b)])
                if blk >= NL * NBLK_A and NSEQ > 1:
                    dma(wbf_d[blk - NL * NBLK_A], WB[:, b * 1024:(b + 1) * 1024], [B("wb", b)], [B("wdram", blk)])
            wstate["use"] += 1
            return WB[:, b * 1024:(b + 1) * 1024].rearrange("p (k c) -> p k c", k=8), B("wb", b)

        for dst, src, key in ((MASK, mask_d, "mask"), (CB, cb_d, "cb"), (IDENT, ident_d, "ident"), (NRM, nrm_d, "nrm"),
                              (QKG, qkg_d, "qkg"), (ES, sink_d, "es"), (PSC, psc_d, "psc"), (EDGE, edge_d, "edge"),
                              (BADA, bada_d, "bada"), (CT, cT_d, "ct"), (WPF, wpool_d, "wpf")):
            dma(dst[:], src, [], [B(key)])
        act(ES[:], ES[:], AF.Exp, [B("es")], [B("es")])
        cp("dve", WPB[:], WPF[:], [B("wpf")], [B("wpb")])
        act(SCT[:], CT[:], AF.Silu, [B("ct")], [B("sct")])
        SCTv = SCT[:].rearrange("p (k b) -> p k b", k=8)
        for l in range(NL):
            for m in range(48):
                wv, wb = w_get()
                P, Pb = ps("A")
                for kc in range(8):
                    mm(P[:, 0:NB], wv[:, kc, :], SCTv[:, kc, :], kc == 0, kc == 7, [wb, B("sct")], [Pb])
                ts("dve", MODv[:, l, m // 8, m % 8, :], P[:, 0:NB], BADA[:, l * 48 + m:l * 48 + m + 1], ALU.add,
                   [Pb, B("bada")], [B("mod")])
        for l in range(NL):
            for w in range(2):
                for col in range(NB):
                    stt(GSv[:, l, w, :, col], MODv[:, l, 1 + 3 * w, :, col], 1.0, NRMv[:, l, w, :], ALU.add, ALU.mult,
                        [B("mod"), B("nrm")], [B("gs")])

        def modvec(l, kind, c, col):
            return MODv[:, l, kind, c, col:col + 1]

        FULL = {"v": True}

        def CHE(full=None):
            f_ = FULL["v"] if full is None else full
            return [(ti, c) for ti, c in enumerate(CH) if f_ or ti > 0]

        def modulate(l, w, b):
            for ti, (t0, n) in CHE():
                col = NSEQ if ti == 0 else b
                P, Pb = ps("B")
                for kc in range(8):
                    sq, sqb = btmp()
                    act(sq[:, :n], XT[:, kc, t0:t0 + n], AF.Square, [B("xT", kc, ti)] + UX, [sqb])
                    mm(P[:, :n], ONESb, sq[:, :n], kc == 0, kc == 7, [sqb, B("cb")], [Pb])
                ln, lnb = ftmp()
                act(ln[:, :n], P[:, :n], AF.Ln, [Pb], [lnb], scale=1.0 / 1024, bias=EPS)
                rs, rsb = RSM, B("rsm")
                act(rs[:, :n], ln[:, :n], AF.Exp, [lnb], [rsb], scale=-0.5)
                for kc in range(8):
                    t, tb = ftmp()
                    tt("dve", t[:, :n], XT[:, kc, t0:t0 + n], rs[:, :n], ALU.mult, [B("xT", kc, ti), rsb] + UX, [tb])
                    act(HT[:, kc, t0:t0 + n], t[:, :n], AF.Identity, [tb, B("gs"), B("mod")], [B("hT", kc, ti)],
                        scale=GSv[:, l, w, kc, col:col + 1], bias=modvec(l, 3 * w, kc, col))

        def proj_acc(P, Pb, wv, wb, ti):
            t0, n = CH[ti]
            for kc in range(8):
                mm(P[:, :n], wv[:, kc, :], HT[:, kc, t0:t0 + n], kc == 0, kc == 7, [wb, B("hT", kc, ti)], [Pb])

        def qk_chunk(wv, wb, l, gidx, dest, dkey, full=None):
            g = QKG[:, l * 4 + gidx:l * 4 + gidx + 1]
            for ti, (t0, n) in CHE(full):
                A, Ab = ps("A")
                for kc in range(8):
                    mm(A[:, :n], wv[:, kc, :], HT[:, kc, t0:t0 + n], kc == 0, kc == 7, [wb, B("hT", kc, ti)], [Ab])
                    if kc == 3:
                        yield
                sq, sqb = btmp()
                act(sq[:, :n], A[:, :n], AF.Square, [Ab], [sqb])
                yield
                Bp, Bb = ps("B")
                mm(Bp[:, :n], BOb, sq[:, :n], True, True, [sqb, B("cb")], [Bb])
                ln, lnb = ftmp()
                act(ln[:, :n], Bp[:, :n], AF.Ln, [Bb], [lnb], scale=1.0 / 64, bias=EPS)
                rs, rsb = ftmp()
                act(rs[:, :n], ln[:, :n], AF.Exp, [lnb], [rsb], scale=-0.5)
                if ti == 0:
                    stt(dest[:, t0:t0 + n], A[:, :n], g, rs[:, :n], ALU.mult, ALU.mult, [Ab, rsb, B("qkg")] + UX,
                        [dkey(ti)])
                    yield
                else:
                    l0 = t0 - CTX
                    qn, qnb = btmp()
                    stt(qn[:, :n], A[:, :n], g, rs[:, :n], ALU.mult, ALU.mult, [Ab, rsb, B("qkg")], [qnb])
                    yield
                    C, Cb = ps("B")
                    mm(C[:, :n], PERMb, qn[:, :n], True, True, [qnb, B("cb")], [Cb])
                    t1, t1b = ftmp()
                    tt("pool", t1[:, :n], qn[:, :n], ROPE_C[:, l0:l0 + n], ALU.mult, [qnb, B("rope")] + UY, [t1b])
                    t2, t2b = ftmp()
                    tt("dve", t2[:, :n], C[:, :n], ROPE_S[:, l0:l0 + n], ALU.mult, [Cb, B("rope")] + UY, [t2b])
                    tt("pool", dest[:, t0:t0 + n], t1[:, :n], t2[:, :n], ALU.add, [t1b, t2b] + UX, [dkey(ti)])
                    yield

        def drain(gen):
            for _ in gen:
                pass

        def interleave(main, side, ratio, which=0):
            if not ILV[which] and side is not None:
                drain(side)
                side = None
            for _ in main:
                for _k in range(ratio):
                    if side is not None and next(side, "end") == "end":
                        side = None
            if side is not None:
                drain(side)

        def v_block(wv, wb, which):
            base = which * 192
            for g0 in range(0, NTT, 4):
                ng = min(4, NTT - g0)
                P, Pb = ps("O")
                for j in range(ng):
                    tk = g0 + j
                    ti = chunk_of_tile(tk)
                    for kc in range(8):
                        mm(P[:, j * 128:(j + 1) * 128], HT[:, kc, tk * 128:(tk + 1) * 128], wv[:, kc, :], kc == 0, kc == 7,
                           [wb, B("hT", kc, ti)], [Pb])
                Pv = P[:, 0:ng * 128].rearrange("p (j c) -> p j c", c=128)
                vb = [B("va", g0 + j) for j in range(ng)]
                act(VA[:, g0:g0 + ng, base:base + 64], Pv[:, :, 0:64], AF.Copy, [Pb] + UX, vb)
                act(VA[:, g0:g0 + ng, base + 128:base + 192], Pv[:, :, 64:128], AF.Copy, [Pb] + UX, vb)
                yield

        def upos(t):
            return t + 8 if t < CTX else t + 24

        def u_block(wv, wb, uc):
            for ti, (t0, n) in CHE():
                A, Ab = ps("O")
                proj_acc(A, Ab, wv, wb, ti)
                a = upos(t0)
                act(UT[:, uc, a:a + n], A[:, :n], AF.Copy, [Ab] + UY, [B("ut", uc, ti)])
                yield

        def pooling(l, uc):
            wins = (2, 4) if uc == 0 else (8, 16)
            for ti, (t0, n) in CHE():
                a = upos(t0)
                ub = [B("ut", uc, k) for k in range(max(0, ti - 1), min(NCH, ti + 2))] + UY
                b2, b2b = ftmp()
                tt("pool", b2[:, 0:n + 14], UT[:, uc, a - 8:a + n + 6], UT[:, uc, a - 7:a + n + 7], ALU.add, ub, [b2b])
                b4, b4b = ftmp()
                tt("pool", b4[:, 0:n + 12], b2[:, 0:n + 12], b2[:, 2:n + 14], ALU.add, [b2b], [b4b])
                if uc == 0:
                    srcs = ((b2, b2b, 7), (b4, b4b, 6))
                else:
                    b8, b8b = ftmp()
                    tt("pool", b8[:, 0:n + 8], b4[:, 0:n + 8], b4[:, 4:n + 12], ALU.add, [b4b], [b8b])
                    b16, b16b = ftmp()
                    tt("pool", b16[:, 0:n], b8[:, 0:n], b8[:, 8:n + 8], ALU.add, [b8b], [b16b])
                    srcs = ((b8, b8b, 4), (b16, b16b, 0))
                pl, plb = btmp()
                for h in range(2):
                    rows = slice(h * 64, h * 64 + 64)
                    bx, bxb, off = srcs[h]
                    stt(pl[rows, :n], bx[rows, off:off + n], 1.0 / wins[h], UT[rows, uc, a:a + n], ALU.mult, ALU.subtract,
                        [bxb] + ub, [plb])
                edges = []
                if ti <= 1:
                    edges.append((0, 0))
                if ti == 0 or ti == NCH - 1:
                    edges.append((n - 8, 8))
                for (c0, e0) in edges:
                    e, eb = ftmp()
                    for h in range(2):
                        rows = slice(h * 64, h * 64 + 64)
                        bx, bxb, off = srcs[h]
                        tt("dve", e[rows, 0:8], bx[rows, off + c0:off + c0 + 8], EDGE[rows, uc * 16 + e0:uc * 16 + e0 + 8],
                           ALU.mult, [bxb, B("edge")], [eb])
                    tt("dve", pl[:, c0:c0 + 8], e[:, 0:8], UT[:, uc, a + c0:a + c0 + 8], ALU.subtract, [eb, plb] + ub, [plb])
                M, Mb = ps("A")
                mm(M[:, :n], WPB[:, (l * 2 + uc) * 128:(l * 2 + uc + 1) * 128], pl[:, :n], True, True, [plb, B("wpb")], [Mb])
                ts("dve", OT[:, 3 + uc, t0:t0 + n], M[:, :n], PSC[:, l * 2 + uc:l * 2 + uc + 1], ALU.mult,
                   [Mb, B("psc")] + UX, [B("oT", 3 + uc, ti)])

        def attention(l, typ, j, qbuf):
            orow = j if typ == 0 else 5 + j
            for qi, (q0, nq) in CHE():
                if qi == 0:
                    kts = [(0, None), (1, None)]
                elif typ == 0:
                    kts = [(kt, None) for kt in range(NTT)]
                else:
                    i0 = 4 * (qi - 1)
                    kts = [(0, None), (1, None)] + [(2 + jj, jj - i0) for jj in range(max(0, i0 - 1), min(NLT, i0 + 5))]
                nk = len(kts)

                def crange(d):
                    if d is None:
                        return 0, nq
                    return max(0, 128 * (d - 1)), min(nq, 128 * (d + 2))

                strm = []
                for hp in range(2):
                    O, Ob = ps("O")
                    strm.append(dict(hp=hp, prow=slice(hp * 64, hp * 64 + 64), drow=slice(64 - hp * 64, 128 - hp * 64),
                                     head=j + 3 * hp, vbase=typ * 192 + hp * 64, O=O, Ob=Ob, pend=[]))

                def pv(sm, idx):
                    kt, pt, ptb, c0, c1 = sm["pend"][idx]
                    mm(sm["O"][:, c0:c1], VA[:, kt, sm["vbase"]:sm["vbase"] + 128], pt[:, c0:c1], idx == 0, idx == nk - 1,
                       [ptb, B("va", kt)] + UX, [sm["Ob"]])

                for idx, (kt, d) in enumerate(kts):
                    c0, c1 = crange(d)
                    for sm in strm:
                        prow = sm["prow"]
                        Sp, Sb = ps("S")
                        mm(Sp[:, c0:c1], KT[prow, typ, kt * 128:(kt + 1) * 128], QT[prow, qbuf, q0 + c0:q0 + c1], True, True,
                           [B("kT", typ, chunk_of_tile(kt)), B("qT", qbuf, qi)] + UX, [Sb])
                        pt, ptb = ptmp()
                        act(pt[:, c0:c1], Sp[:, c0:c1], AF.Exp, [Sb], [ptb], scale=0.125)
                        if d is not None:
                            tt("pool", pt[:, c0:c1], pt[:, c0:c1], MASK[:, (d + 1) * 512 + c0:(d + 1) * 512 + c1], ALU.mult,
                               [ptb, B("mask")], [ptb])
                        sm["pend"].append((kt, pt, ptb, c0, c1))
                    if idx >= 1:
                        for sm in strm:
                            pv(sm, idx - 1)
                    if idx % 4 == 3 and idx < nk - 1:
                        yield
                for sm in strm:
                    pv(sm, nk - 1)
                yield
                evs = []
                for sm in strm:
                    ob, obb = ftmp()
                    cp("dve", ob[:, :nq], sm["O"][:, :nq], [sm["Ob"]], [obb])
                    evs.append((ob, obb))
                for sm, (ob, obb) in zip(strm, evs):
                    prow, drow = sm["prow"], sm["drow"]
                    rec, recb = ftmp()
                    if typ == 1:
                        ts("dve", rec[prow, :nq], ob[drow, :nq], ES[prow, l * 6 + sm["head"]:l * 6 + sm["head"] + 1], ALU.add,
                           [obb, B("es")], [recb])
                        recip(rec[prow, :nq], rec[prow, :nq], [recb], [recb])
                    else:
                        recip(rec[prow, :nq], ob[drow, :nq], [obb], [recb])
                    tt("dve", OT[prow, orow, q0:q0 + nq], ob[prow, :nq], rec[prow, :nq], ALU.mult, [obb, recb] + UX,
                       [B("oT", orow, qi)])

        def merge(l):
            for c in range(8):
                wg = [w_get() for _ in range(3)]
                wbr = w_get()
                for ti, (t0, n) in CHE():
                    ms = []
                    for br, (r0, nr) in enumerate(((0, 3), (3, 2), (5, 3))):
                        G, Gb = ps("S")
                        proj_acc(G, Gb, wg[br][0], wg[br][1], ti)
                        sg, sgb = ftmp()
                        act(sg[:, :n], G[:, :n], AF.Sigmoid, [Gb], [sgb])
                        Bp, Bb = ps("AO")
                        for r in range(nr):
                            mm(Bp[:, :n], wbr[0][:, r0 + r, :], OT[:, r0 + r, t0:t0 + n], r == 0, r == nr - 1,
                               [wbr[1], B("oT", r0 + r, ti)] + UX, [Bb])
                        m, mb = ftmp()
                        tt("dve", m[:, :n], Bp[:, :n], sg[:, :n], ALU.mult, [Bb, sgb], [mb])
                        ms.append((m, mb))
                    s, sbf = ftmp()
                    tt("pool", s[:, :n], ms[0][0][:, :n], ms[1][0][:, :n], ALU.add, [ms[0][1], ms[1][1]], [sbf])
                    tt("pool", YT[:, c, t0:t0 + n], s[:, :n], ms[2][0][:, :n], ALU.add, [sbf, ms[2][1]] + UY,
                       [B("yT", c, ti)])

        def wout(l, b):
            join([], UX)
            for c in range(8):
                dma(XT[:, c, :], xs_d[c], UX, [B("xT", c, ti) for ti in range(NCH)])
            for c in range(8):
                wv, wb = w_get()
                for ti, (t0, n) in CHE():
                    col = NSEQ if ti == 0 else b
                    P, Pb = ps("AO")
                    for kc in range(8):
                        mm(P[:, :n], wv[:, kc, :], YT[:, kc, t0:t0 + n], kc == 0, kc == 7, [wb, B("yT", kc, ti)] + UY, [Pb])
                    stt(XT[:, c, t0:t0 + n], P[:, :n], modvec(l, 2, c, col), XT[:, c, t0:t0 + n], ALU.mult, ALU.add,
                        [Pb, B("mod"), B("xT", c, ti)] + UX, [B("xT", c, ti)])

        def mlp(l, b):
            for g in range(4):
                for kh in range(8):
                    wv, wb = w_get()
                    for ti, (t0, n) in CHE():
                        P, Pb = ps("S")
                        proj_acc(P, Pb, wv, wb, ti)
                        r, rb = ftmp()
                        act(r[:, :n], P[:, :n], AF.Relu, [Pb], [rb])
                        tt("dve", YT[:, kh, t0:t0 + n], r[:, :n], r[:, :n], ALU.mult, [rb] + UY, [B("yT", kh, ti)])
                for c in range(8):
                    wv, wb = w_get()
                    for ti, (t0, n) in CHE():
                        col = NSEQ if ti == 0 else b
                        P, Pb = ps("AO")
                        for kh in range(8):
                            mm(P[:, :n], wv[:, kh, :], YT[:, kh, t0:t0 + n], kh == 0, kh == 7, [wb, B("yT", kh, ti)] + UY, [Pb])
                        stt(XT[:, c, t0:t0 + n], P[:, :n], modvec(l, 5, c, col), XT[:, c, t0:t0 + n], ALU.mult, ALU.add,
                            [Pb, B("mod"), B("xT", c, ti)] + UX, [B("xT", c, ti)])

        def load_seq(b):
            for tk in range(NTT):
                ti = chunk_of_tile(tk)
                for half in range(2):
                    f, fb = ftmp()
                    cs = slice(half * 512, (half + 1) * 512)
                    src = ctx_d[b, tk * 128:(tk + 1) * 128, cs] if tk < 2 else x_d[b, (tk - 2) * 128:(tk - 1) * 128, cs]
                    dma(f[:, 0:512], src, [], [fb])
                    P, Pb = ps("AO")
                    for q in range(4):
                        p.op("pe", (lambda o=P[:, q * 128:(q + 1) * 128], i=f[:, q * 128:(q + 1) * 128]:
                                    nc.tensor.transpose(o, i, IDENT[:])), [fb, B("ident")], [Pb])
                    Pv = P[:].rearrange("p (j c) -> p j c", c=128)
                    cp("dve", XT[:, half * 4:half * 4 + 4, tk * 128:(tk + 1) * 128], Pv, [Pb] + UX,
                       [B("xT", half * 4 + q, ti) for q in range(4)])

        def store_seq(b):
            for lt in range(NLT):
                tk = 2 + lt
                ti = chunk_of_tile(tk)
                for half in range(2):
                    P, Pb = ps("AO")
                    for q in range(4):
                        kc = half * 4 + q
                        p.op("pe", (lambda o=P[:, q * 128:(q + 1) * 128], i=XT[:, kc, tk * 128:(tk + 1) * 128]:
                                    nc.tensor.transpose(o, i, IDENT[:])), [B("xT", kc, ti), B("ident")] + UX, [Pb])
                    f, fb = ftmp()
                    cp("dve", f[:, 0:512], P[:], [Pb], [fb])
                    dma(out_d[b, lt * 128:(lt + 1) * 128, half * 512:(half + 1) * 512], f[:, 0:512], [fb], [])

        for b in range(NSEQ):
            load_seq(b)
            for l in range(NL):
                FULL["v"] = True
                modulate(l, 0, b)
                for c in range(8):
                    dma(xs_d[c], XT[:, c, :], [B("xT", c, ti) for ti in range(NCH)] + UX, [B("spill", c)])
                join([B("spill", c) for c in range(8)] + [B("hT", kc, ti) for kc in range(8) for ti in range(NCH)], UX)
                join([], UY)
                dma(ROPE_C, rope_d[0], UY, [B("rope")])
                dma(ROPE_S, rope_d[1], UY + [B("rope")], [B("rope")])
                for uc in range(2):
                    mset("pool", UT[:, uc, :], 0.0, UY, [B("ut", uc, ti) for ti in range(NCH)])
                for tk in range(NTT):
                    mset("pool", VA[:, tk, 64:128], 1.0, UX, [B("va", tk)])
                    mset("pool", VA[:, tk, 256:320], 1.0, UX, [B("va", tk)])
                def both(*gens):
                    for g_ in gens:
                        yield from g_

                wk, wkb = w_get()
                wva, wvab = w_get()
                wvc, wvcb = w_get()
                interleave(both(v_block(wva, wvab, 0), v_block(wvc, wvcb, 1)),
                           qk_chunk(wk, wkb, l, 1, KT[:, 0, :], lambda ti: B("kT", 0, ti), full=True), 2)
                wk, wkb = w_get()
                wu0, wu0b = w_get()
                wu1, wu1b = w_get()
                FULL["v"] = l < NL - 1
                interleave(both(u_block(wu0, wu0b, 0), u_block(wu1, wu1b, 1)),
                           qk_chunk(wk, wkb, l, 3, KT[:, 1, :], lambda ti: B("kT", 1, ti), full=True), 2)
                for uc in range(2):
                    pooling(l, uc)
                order_q = [(typ, j) for typ in range(2) for j in range(3)]
                wv, wb = w_get()
                drain(qk_chunk(wv, wb, l, 0, QT[:, 0, :], lambda ti: B("qT", 0, ti)))
                for idx, (typ, j) in enumerate(order_q):
                    qb = idx % 2
                    side = None
                    if idx + 1 < len(order_q):
                        wv, wb = w_get()
                        nqb = (idx + 1) % 2
                        side = qk_chunk(wv, wb, l, 2 * order_q[idx + 1][0], QT[:, nqb, :], lambda ti, nqb=nqb: B("qT", nqb, ti))
                    interleave(attention(l, typ, j, qb), side, 1 if typ == 0 else 2, 1)
                if stop_after == "attn":
                    break
                join([], UY)
                merge(l)
                if stop_after == "merge":
                    break
                wout(l, b)
                if stop_after == "wout":
                    break
                modulate(l, 1, b)
                mlp(l, b)
                if stop_after == "layer0":
                    break
            if stop_after is None:
                store_seq(b)

        p.emit(st)
    return nc


def _blk(w, rows, cols):
    sub = w[np.ix_(rows, cols)] if not isinstance(rows, slice) else w[rows][:, cols]
    return np.ascontiguousarray(sub.reshape(8, 128, 128).transpose(1, 0, 2)).reshape(128, 1024)


def _layer_blocks(l, w_in, w_br_a, w_br_b, w_br_c, w_out, w_mlp1, w_mlp2):
    ar = np.arange
    allr = slice(0, 1024)
    wi = w_in[l]
    qa0, ka0, va0, qc0, kc0, vc0, u0, ga0 = 0, 384, 512, 640, 1024, 1152, 1280, 1536
    blocks = []

    def qcols(base, j):
        return np.concatenate([base + j * 64 + ar(64), base + (j + 3) * 64 + ar(64)])

    blocks.append(_blk(wi, allr, ka0 + ar(128)))
    blocks.append(_blk(wi, allr, va0 + ar(128)))
    blocks.append(_blk(wi, allr, vc0 + ar(128)))
    blocks.append(_blk(wi, allr, kc0 + ar(128)))
    blocks.append(_blk(wi, allr, u0 + ar(128)))
    blocks.append(_blk(wi, allr, u0 + 128 + ar(128)))
    for j in range(3):
        blocks.append(_blk(wi, allr, qcols(qa0, j)))
    for j in range(3):
        blocks.append(_blk(wi, allr, qcols(qc0, j)))
    rows_a = np.concatenate([qcols(0, j) for j in range(3)])
    wbr = np.concatenate([w_br_a[l][rows_a], w_br_b[l], w_br_c[l][rows_a]], axis=0)
    for c in range(8):
        for br in range(3):
            blocks.append(_blk(wi, allr, ga0 + br * 1024 + c * 128 + ar(128)))
        blocks.append(_blk(wbr, allr, c * 128 + ar(128)))
    for c in range(8):
        blocks.append(_blk(w_out[l], allr, c * 128 + ar(128)))
    for g in range(4):
        for kh in range(8):
            blocks.append(_blk(w_mlp1[l], allr, (g * 8 + kh) * 128 + ar(128)))
        for c in range(8):
            blocks.append(_blk(w_mlp2[l], slice(g * 1024, (g + 1) * 1024), c * 128 + ar(128)))
    assert len(blocks) == NBLK_L
    return blocks


def _consts(S):
    ar = np.arange
    t = ar(S, dtype=np.float32)
    r = np.floor(t / 64.0).astype(np.float32)
    col = (t - r * 64.0).astype(np.float32)
    inv = (1.0 / (10000.0 ** (ar(0, 32, 2, dtype=np.float32) / 32.0))).astype(np.float32)
    ang = np.concatenate([r[:, None] * inv, col[:, None] * inv], axis=-1).astype(np.float32)
    cos, sin = np.cos(ang).astype(np.float32), np.sin(ang).astype(np.float32)
    d = ar(128) % 64
    C = cos[:, d // 2].T.copy()
    Sg = (sin[:, d // 2] * np.where(d % 2 == 0, -1.0, 1.0)[None, :]).T.copy()
    rope = np.stack([C, Sg]).astype(np.float32)
    k = ar(128)[:, None]
    q = ar(512)[None, :]
    mask = np.concatenate([(np.abs(q - k - 128 * dd) <= 128).astype(np.float32) for dd in range(-1, 5)], axis=1)
    perm = np.zeros((128, 128), np.float32)
    perm[ar(128) ^ 1, ar(128)] = 1.0
    bo = (ar(128)[:, None] // 64 == ar(128)[None, :] // 64).astype(np.float32)
    ones = np.ones((128, 128), np.float32)
    cb = np.concatenate([perm, bo, ones], axis=1)
    ident = np.eye(128, dtype=np.float32)
    edge = np.zeros((128, 2, 16), np.float32)
    for uc in range(2):
        for h in range(2):
            w = (2, 4, 8, 16)[uc * 2 + h]
            for i in range(8):
                edge[h * 64:(h + 1) * 64, uc, i] = 1.0 / min(w, i + w // 2)
                edge[h * 64:(h + 1) * 64, uc, 8 + i] = 1.0 / min(w, 8 - i + w // 2)
    return dict(rope=rope, mask=mask.astype(ml_dtypes.bfloat16), cb=cb.astype(ml_dtypes.bfloat16), ident=ident,
                edge=edge.reshape(128, 32))


def prep_shared(NL, S, w_ada, b_ada, norm1, norm2, w_in, q_norm_a, k_norm_a, q_norm_c, k_norm_c, sink_c, w_pool,
                pool_scale, w_br_a, w_br_b, w_br_c, w_out, w_mlp1, w_mlp2):
    ar = np.arange
    f = lambda a: np.asarray(a, dtype=np.float32)
    w_ada, b_ada, w_in, w_out, w_mlp1, w_mlp2 = f(w_ada), f(b_ada), f(w_in), f(w_out), f(w_mlp1), f(w_mlp2)
    w_br_a, w_br_b, w_br_c = f(w_br_a), f(w_br_b), f(w_br_c)
    blocks = []
    for l in range(NL):
        for m in range(48):
            blocks.append(_blk(w_ada[l], slice(0, 1024), m * 128 + ar(128)))
    for l in range(NL):
        blocks += _layer_blocks(l, w_in, w_br_a, w_br_b, w_br_c, w_out, w_mlp1, w_mlp2)
    W = np.stack(blocks)
    sh = dict(W=W)
    sh["bada"] = np.ascontiguousarray(b_ada[:NL].reshape(NL, 48, 128).transpose(2, 0, 1)).reshape(128, NL * 48)
    nrm = np.stack([f(norm1)[:NL], f(norm2)[:NL]], axis=1)
    sh["nrm"] = np.ascontiguousarray(nrm.reshape(NL, 2, 8, 128).transpose(3, 0, 1, 2)).reshape(128, NL * 16)
    g4 = np.stack([f(q_norm_a)[:NL], f(k_norm_a)[:NL], f(q_norm_c)[:NL], f(k_norm_c)[:NL]], axis=1)
    sh["qkg"] = np.ascontiguousarray(np.concatenate([g4, g4], axis=2).transpose(2, 0, 1)).reshape(128, NL * 4)
    sh["sink"] = np.ascontiguousarray(np.broadcast_to(f(sink_c)[:NL].reshape(1, NL * 6), (128, NL * 6)))
    sh["psc"] = np.ascontiguousarray(f(pool_scale)[:NL].reshape(NL, 2, 128).transpose(2, 0, 1)).reshape(128, NL * 2)
    wp = np.zeros((128, NL, 2, 128), np.float32)
    wpl = f(w_pool)
    for l in range(NL):
        for uc in range(2):
            for h in range(2):
                wp[h * 64:(h + 1) * 64, l, uc, h * 64:(h + 1) * 64] = wpl[l, uc * 2 + h]
    sh["wpool"] = wp.reshape(128, NL * 256)
    sh.update(_consts(S))
    return sh


def prep_core(c_rows, c_ctx):
    cc = np.concatenate([np.asarray(c_rows, np.float32), np.asarray(c_ctx, np.float32)[None]], axis=0)
    NBc = cc.shape[0]
    return np.ascontiguousarray(cc.reshape(NBc, 8, 128).transpose(2, 1, 0)).reshape(128, 8 * NBc)


_NC_CACHE = {}


def kernel(x, c, ctx, c_ctx, w_ada, b_ada, norm1, norm2, w_in, q_norm_a, k_norm_a, q_norm_c, k_norm_c,
           sink_c, w_pool, pool_scale, w_br_a, w_br_b, w_br_c, w_out, w_mlp1, w_mlp2):
    x = np.asarray(x, np.float32)
    c = np.asarray(c, np.float32)
    ctx = np.asarray(ctx, np.float32)
    NCORES = 8
    Bt, S, _ = x.shape
    NL = np.asarray(w_in).shape[0]
    NSEQ = Bt // NCORES
    key = (NSEQ, S, NL)
    if key not in _NC_CACHE:
        _NC_CACHE[key] = build(NSEQ, S, NL)
    nc = _NC_CACHE[key]
    sh = prep_shared(NL, S, w_ada, b_ada, norm1, norm2, w_in, q_norm_a, k_norm_a, q_norm_c, k_norm_c, sink_c, w_pool,
                     pool_scale, w_br_a, w_br_b, w_br_c, w_out, w_mlp1, w_mlp2)
    in_maps = []
    for i in range(NCORES):
        sl = slice(i * NSEQ, (i + 1) * NSEQ)
        m = dict(sh)
        m["x"] = np.ascontiguousarray(x[sl])
        m["ctx"] = np.ascontiguousarray(ctx[sl])
        m["cT"] = prep_core(c[sl], c_ctx)
        in_maps.append(m)
    res = run_bass_kernel_spmd(nc, in_maps, core_ids=list(range(NCORES)))
    return np.concatenate([r["out"] for r in res.results], axis=0).astype(np.float32)
```

```python
import contextlib
import numpy as np
import ml_dtypes
import concourse.bass as bass
import concourse.mybir as mybir
from concourse.bass_utils import run_bass_kernel_spmd

F32 = mybir.dt.float32
BF16 = mybir.dt.bfloat16
AF = mybir.ActivationFunctionType
ALU = mybir.AluOpType

NDMA_SLOTS = 8
CTX = 256
EPS = 1e-6
NBLK_L = 116
NBLK_A = 48
NOCACHE = False
ILV = [True, True]


class Buf:
    __slots__ = ("name", "lw", "rd")

    def __init__(self, name):
        self.name = name
        self.lw = None
        self.rd = {}


class Op:
    __slots__ = ("stream", "fn", "deps", "signal", "is_dma", "semkey", "sigval")

    def __init__(self, stream, fn, is_dma):
        self.stream = stream
        self.fn = fn
        self.deps = []
        self.signal = False
        self.is_dma = is_dma
        self.semkey = None
        self.sigval = None


class Prog:
    def __init__(self, nc):
        self.nc = nc
        self.ops = []
        self.bufs = {}
        self.dma_count = {}
        self.dma_ops = {}
        self.streams = ("pe", "act", "dve", "pool", "sp")

    def B(self, *key):
        b = self.bufs.get(key)
        if b is None:
            b = Buf(key)
            self.bufs[key] = b
        return b

    def _add(self, op, reads, writes):
        deps = {}

        def need(p, raw):
            if p is None:
                return
            if (not p.is_dma) and (not op.is_dma) and p.stream == op.stream:
                if op.stream == "pe":
                    return
            deps[id(p)] = p

        for r in reads:
            need(r.lw, True)
        for w in writes:
            need(w.lw, False)
            for q in w.rd.values():
                need(q, False)
        op.deps = list(deps.values())
        for q in op.deps:
            q.signal = True
        for r in reads:
            r.rd[id(op) if op.is_dma else op.stream] = op
        for w in writes:
            w.lw = op
            w.rd = {}
        self.ops.append(op)
        return op

    def op(self, stream, fn, reads=(), writes=()):
        return self._add(Op(stream, fn, False), reads, writes)

    def dma(self, stream, fn, reads=(), writes=()):
        op = Op(stream, fn, True)
        n = self.dma_count.get(stream, 0)
        self.dma_count[stream] = n + 1
        op.semkey = ("dma", stream, n % NDMA_SLOTS)
        op.sigval = 16 * (n // NDMA_SLOTS + 1)
        op.signal = True
        lst = self.dma_ops.setdefault(stream, [])
        self._add(op, reads, writes)
        if n >= NDMA_SLOTS:
            op.deps.append(lst[n - NDMA_SLOTS])
        lst.append(op)
        return op

    def emit(self, stack):
        nc = self.nc
        sems = {}
        cnt = {}
        for op in self.ops:
            if (not op.is_dma) and op.signal:
                cnt[op.stream] = cnt.get(op.stream, 0) + 1
                op.semkey = ("eng", op.stream)
                op.sigval = cnt[op.stream]
        keys = set(op.semkey for op in self.ops if op.semkey is not None)
        for k in sorted(keys, key=str):
            sems[k] = stack.enter_context(nc.semaphore("s_" + "_".join(str(x) for x in k)))
        final = {}
        for op in self.ops:
            if op.signal and final.get(op.semkey, 0) < op.sigval:
                final[op.semkey] = op.sigval
        per = {s: [op for op in self.ops if op.stream == s] for s in self.streams}

        def body(stream):
            def f(eng):
                w = {}
                for op in per[stream]:
                    need = {}
                    for q in op.deps:
                        if need.get(q.semkey, 0) < q.sigval:
                            need[q.semkey] = q.sigval
                    for k, v in need.items():
                        if w.get(k, 0) < v:
                            eng.wait_ge(sems[k], v)
                            w[k] = v
                    inst = op.fn()
                    if op.signal:
                        inst.then_inc(sems[op.semkey], 16 if op.is_dma else 1)
                if stream == "sp":
                    for k, v in final.items():
                        eng.wait_ge(sems[k], v)
            return f

        with nc.Block() as block:
            block.tensor(body("pe"))
            block.scalar(body("act"))
            block.vector(body("dve"))
            block.gpsimd(body("pool"))
            block.sync(body("sp"))


def build(NSEQ, S, NL, stop_after=None):
    T = CTX + S
    NTT = T // 128
    NLT = S // 128
    NQC = S // 512
    NB = NSEQ + 1
    CH = [(0, 256)] + [(256 + 512 * i, 512) for i in range(NQC)]
    NCH = len(CH)
    FW = 544

    def chunk_of_tile(tt):
        return 0 if tt < 2 else 1 + (tt - 2) // 4

    nc = bass.Bass("TRN2", target_bir_lowering=False)

    def din(name, shape, dt=F32):
        return nc.dram_tensor(name, list(shape), dt, kind="ExternalInput").ap()

    x_d = din("x", [NSEQ, S, 1024])
    ctx_d = din("ctx", [NSEQ, CTX, 1024])
    W_d = din("W", [NL * NBLK_A + NL * NBLK_L, 128, 1024])
    cT_d = din("cT", [128, 8 * NB])
    bada_d = din("bada", [128, NL * 48])
    nrm_d = din("nrm", [128, NL * 2 * 8])
    qkg_d = din("qkg", [128, NL * 4])
    sink_d = din("sink", [128, NL * 6])
    psc_d = din("psc", [128, NL * 2])
    edge_d = din("edge", [128, 2 * 16])
    wpool_d = din("wpool", [128, NL * 2 * 128])
    rope_d = din("rope", [2, 128, S])
    ident_d = din("ident", [128, 128])
    mask_d = din("mask", [128, 6 * 512], BF16)
    cb_d = din("cb", [128, 3 * 128], BF16)
    out_d = nc.dram_tensor("out", [NSEQ, S, 1024], F32, kind="ExternalOutput").ap()
    xs_d = nc.dram_tensor("xs", [8, 128, T], F32, kind="Internal").ap()
    wbf_d = nc.dram_tensor("wbf", [NL * NBLK_L, 128, 1024], BF16, kind="Internal").ap()

    st = contextlib.ExitStack()
    with st:
        def sb(name, shape, dt):
            return st.enter_context(nc.sbuf_tensor(name, list(shape), dt))

        XTr = sb("XTr", [128, 8 * T], F32)
        HTt = sb("HT", [128, 8 * T], BF16)
        YTt = sb("YT", [128, 8 * T], BF16)
        NSTG, NWB, NPT, NF, NBT = 3, 6, 4, 7, 4
        STG = sb("STG", [128, NSTG * 1024], F32)
        WB = sb("WB", [128, NWB * 1024], BF16)
        PTt = sb("PT", [128, NPT * 512], BF16)
        MASK = sb("MASK", [128, 6 * 512], BF16)
        CB = sb("CB", [128, 3 * 128], BF16)
        IDENT = sb("IDENT", [128, 128], F32)
        FT = sb("FT", [128, NF * FW], F32)
        BT = sb("BTMP", [128, NBT * 512], BF16)
        MOD = sb("MOD", [128, NL * 48 * NB], F32)
        GS = sb("GS", [128, NL * 2 * 8 * NB], F32)
        NRM = sb("NRM", [128, NL * 16], F32)
        QKG = sb("QKG", [128, NL * 4], F32)
        ES = sb("ES", [128, NL * 6], F32)
        PSC = sb("PSC", [128, NL * 2], F32)
        EDGE = sb("EDGE", [128, 32], F32)
        BADA = sb("BADA", [128, NL * 48], F32)
        CT = sb("CT", [128, 8 * NB], F32)
        SCT = sb("SCT", [128, 8 * NB], BF16)
        WPF = sb("WPF", [128, NL * 256], F32)
        WPB = sb("WPB", [128, NL * 256], BF16)
        DUMMY = sb("DUMMY", [128, 8], F32)
        RSM = sb("RSM", [128, 512], F32)
        PS = [st.enter_context(nc.psum_tensor(f"ps{i}", [128, 512], F32)) for i in range(8)]

        XT = XTr[:].rearrange("p (c t) -> p c t", c=8)
        HT = HTt[:].rearrange("p (c t) -> p c t", c=8)
        YT = YTt[:].rearrange("p (c t) -> p c t", c=8)
        XB = XTr[:].bitcast(BF16)
        OT = XB[:, 0:8 * T].rearrange("p (c t) -> p c t", c=8)
        KT = XB[:, 8 * T:10 * T].rearrange("p (c t) -> p c t", c=2)
        VA = XB[:, 10 * T:13 * T].rearrange("p (k c) -> p k c", c=384)
        QT = XB[:, 13 * T:15 * T].rearrange("p (c t) -> p c t", c=2)
        YF = YTt[:].bitcast(F32)
        ROPE_C = YF[:, 0:S]
        ROPE_S = YF[:, S:2 * S]
        UW = T + 32
        UT = YF[:, 2 * S:2 * S + 2 * UW].rearrange("p (c t) -> p c t", c=2)
        PERMb = CB[:, 0:128]
        BOb = CB[:, 128:256]
        ONESb = CB[:, 256:384]
        MODv = MOD[:].rearrange("p (l k c b) -> p l k c b", l=NL, k=6, c=8)
        GSv = GS[:].rearrange("p (l w c b) -> p l w c b", l=NL, w=2, c=8)
        NRMv = NRM[:].rearrange("p (l w c) -> p l w c", l=NL, w=2)

        p = Prog(nc)
        B = p.B
        UX = [B("UX")]
        UY = [B("UY")]

        def mm(out, lhsT, rhs, start, stop, reads, writes):
            p.op("pe", lambda: nc.tensor.matmul(out, lhsT=lhsT, rhs=rhs, start=start, stop=stop), reads, writes)

        def act(out, in_, func, reads, writes, scale=1.0, bias=0.0):
            p.op("act", lambda: nc.scalar.activation(out=out, in_=in_, func=func, bias=bias, scale=scale), reads, writes)

        def tt(eng, out, in0, in1, op, reads, writes):
            e = nc.vector if eng == "dve" else nc.gpsimd
            p.op(eng, lambda: e.tensor_tensor(out=out, in0=in0, in1=in1, op=op), reads, writes)

        def stt(out, in0, scalar, in1, op0, op1, reads, writes):
            p.op("dve", lambda: nc.vector.scalar_tensor_tensor(out=out, in0=in0, scalar=scalar, in1=in1,
                                                                op0=op0, op1=op1), reads, writes)

        def ts(eng, out, in0, s1, op0, reads, writes):
            e = nc.vector if eng == "dve" else nc.gpsimd
            p.op(eng, lambda: e.tensor_scalar(out=out, in0=in0, scalar1=s1, scalar2=None, op0=op0), reads, writes)

        def recip(out, in_, reads, writes):
            p.op("dve", lambda: nc.vector.reciprocal(out=out, in_=in_), reads, writes)

        def cp(eng, out, in_, reads, writes):
            e = {"dve": nc.vector, "pool": nc.gpsimd}[eng]
            p.op(eng, lambda: e.tensor_copy(out=out, in_=in_), reads, writes)

        def mset(eng, ap, val, reads, writes):
            e = {"dve": nc.vector, "pool": nc.gpsimd}[eng]
            p.op(eng, lambda: e.memset(ap, val), reads, writes)

        def dma(out, in_, reads, writes):
            p.dma("sp", lambda: nc.sync.dma_start(out=out, in_=in_), reads, writes)

        def join(reads, writes):
            p.op("pool", lambda: nc.gpsimd.memset(DUMMY[:, 0:1], 0.0), reads, list(writes) + [B("dummy")])

        rot = {}

        def ring(name, n):
            i = rot.get(name, 0)
            rot[name] = i + 1
            return i % n

        def ps(pool):
            banks = {"A": (0,), "B": (1,), "S": (2, 3, 4, 5), "O": (6, 7), "AO": (0, 1, 6, 7)}[pool]
            i = banks[ring("ps" + pool, len(banks))]
            return PS[i], B("ps", i)

        def ftmp():
            i = ring("ft", NF)
            return FT[:, i * FW:(i + 1) * FW], B("ft", i)

        def btmp():
            i = ring("bt", NBT)
            return BT[:, i * 512:(i + 1) * 512], B("bt", i)

        def ptmp():
            i = ring("pt", NPT)
            return PTt[:, i * 512:(i + 1) * 512], B("pt", i)

        order = [(i, 0) for i in range(NL * NBLK_A)]
        for sq in range(NSEQ):
            order += [(NL * NBLK_A + i, sq) for i in range(NL * NBLK_L)]
        wstate = {"load": 0, "use": 0, "slot": {}}

        def cached(i):
            blk, sq = order[i]
            return blk >= NL * NBLK_A and sq > 0 and not NOCACHE

        def w_prefetch(upto):
            while wstate["load"] < min(upto, len(order)):
                i = wstate["load"]
                blk, sq = order[i]
                if cached(i):
                    b = ring("wb", NWB)
                    wstate["slot"][i] = b
                    dma(WB[:, b * 1024:(b + 1) * 1024], wbf_d[blk - NL * NBLK_A], [B("wdram", blk)], [B("wb", b)])
                else:
                    s_ = ring("stg", NSTG)
                    wstate["slot"][i] = (s_, ring("wb", NWB))
                    dma(STG[:, s_ * 1024:(s_ + 1) * 1024], W_d[blk], [], [B("stg", s_)])
                wstate["load"] += 1

        def w_get():
            i = wstate["use"]
            w_prefetch(i + 3)
            blk, sq = order[i]
            if cached(i):
                b = wstate["slot"].pop(i)
            else:
                s_, b = wstate["slot"].pop(i)
                cp("dve", WB[:, b * 1024:(b + 1) * 1024], STG[:, s_ * 1024:(s_ + 1) * 1024], [B("stg", s_)], [B("wb", b)])
                if blk >= NL * NBLK_A and NSEQ > 1:
                    dma(wbf_d[blk - NL * NBLK_A], WB[:, b * 1024:(b + 1) * 1024], [B("wb", b)], [B("wdram", blk)])
            wstate["use"] += 1
            return WB[:, b * 1024:(b + 1) * 1024].rearrange("p (k c) -> p k c", k=8), B("wb", b)

        for dst, src, key in ((MASK, mask_d, "mask"), (CB, cb_d, "cb"), (IDENT, ident_d, "ident"), (NRM, nrm_d, "nrm"),
                              (QKG, qkg_d, "qkg"), (ES, sink_d, "es"), (PSC, psc_d, "psc"), (EDGE, edge_d, "edge"),
                              (BADA, bada_d, "bada"), (CT, cT_d, "ct"), (WPF, wpool_d, "wpf")):
            dma(dst[:], src, [], [B(key)])
        act(ES[:], ES[:], AF.Exp, [B("es")], [B("es")])
        cp("dve", WPB[:], WPF[:], [B("wpf")], [B("wpb")])
        act(SCT[:], CT[:], AF.Silu, [B("ct")], [B("sct")])
        SCTv = SCT[:].rearrange("p (k b) -> p k b", k=8)
        for l in range(NL):
            for m in range(48):
                wv, wb = w_get()
                P, Pb = ps("A")
                for kc in range(8):
                    mm(P[:, 0:NB], wv[:, kc, :], SCTv[:, kc, :], kc == 0, kc == 7, [wb, B("sct")], [Pb])
                ts("dve", MODv[:, l, m // 8, m % 8, :], P[:, 0:NB], BADA[:, l * 48 + m:l * 48 + m + 1], ALU.add,
                   [Pb, B("bada")], [B("mod")])
        for l in range(NL):
            for w in range(2):
                for col in range(NB):
                    stt(GSv[:, l, w, :, col], MODv[:, l, 1 + 3 * w, :, col], 1.0, NRMv[:, l, w, :], ALU.add, ALU.mult,
                        [B("mod"), B("nrm")], [B("gs")])

        def modvec(l, kind, c, col):
            return MODv[:, l, kind, c, col:col + 1]

        FULL = {"v": True}

        def CHE(full=None):
            f_ = FULL["v"] if full is None else full
            return [(ti, c) for ti, c in enumerate(CH) if f_ or ti > 0]

        def modulate(l, w, b):
            for ti, (t0, n) in CHE():
                col = NSEQ if ti == 0 else b
                P, Pb = ps("B")
                for kc in range(8):
                    sq, sqb = btmp()
                    act(sq[:, :n], XT[:, kc, t0:t0 + n], AF.Square, [B("xT", kc, ti)] + UX, [sqb])
                    mm(P[:, :n], ONESb, sq[:, :n], kc == 0, kc == 7, [sqb, B("cb")], [Pb])
                ln, lnb = ftmp()
                act(ln[:, :n], P[:, :n], AF.Ln, [Pb], [lnb], scale=1.0 / 1024, bias=EPS)
                rs, rsb = RSM, B("rsm")
                act(rs[:, :n], ln[:, :n], AF.Exp, [lnb], [rsb], scale=-0.5)
                for kc in range(8):
                    t, tb = ftmp()
                    tt("dve", t[:, :n], XT[:, kc, t0:t0 + n], rs[:, :n], ALU.mult, [B("xT", kc, ti), rsb] + UX, [tb])
                    act(HT[:, kc, t0:t0 + n], t[:, :n], AF.Identity, [tb, B("gs"), B("mod")], [B("hT", kc, ti)],
                        scale=GSv[:, l, w, kc, col:col + 1], bias=modvec(l, 3 * w, kc, col))

        def proj_acc(P, Pb, wv, wb, ti):
            t0, n = CH[ti]
            for kc in range(8):
                mm(P[:, :n], wv[:, kc, :], HT[:, kc, t0:t0 + n], kc == 0, kc == 7, [wb, B("hT", kc, ti)], [Pb])

        def qk_chunk(wv, wb, l, gidx, dest, dkey, full=None):
            g = QKG[:, l * 4 + gidx:l * 4 + gidx + 1]
            for ti, (t0, n) in CHE(full):
                A, Ab = ps("A")
                for kc in range(8):
                    mm(A[:, :n], wv[:, kc, :], HT[:, kc, t0:t0 + n], kc == 0, kc == 7, [wb, B("hT", kc, ti)], [Ab])
                    if kc == 3:
                        yield
                sq, sqb = btmp()
                act(sq[:, :n], A[:, :n], AF.Square, [Ab], [sqb])
                yield
                Bp, Bb = ps("B")
                mm(Bp[:, :n], BOb, sq[:, :n], True, True, [sqb, B("cb")], [Bb])
                ln, lnb = ftmp()
                act(ln[:, :n], Bp[:, :n], AF.Ln, [Bb], [lnb], scale=1.0 / 64, bias=EPS)
                rs, rsb = ftmp()
                act(rs[:, :n], ln[:, :n], AF.Exp, [lnb], [rsb], scale=-0.5)
                if ti == 0:
                    stt(dest[:, t0:t0 + n], A[:, :n], g, rs[:, :n], ALU.mult, ALU.mult, [Ab, rsb, B("qkg")] + UX,
                        [dkey(ti)])
                    yield
                else:
                    l0 = t0 - CTX
                    qn, qnb = btmp()
                    stt(qn[:, :n], A[:, :n], g, rs[:, :n], ALU.mult, ALU.mult, [Ab, rsb, B("qkg")], [qnb])
                    yield
                    C, Cb = ps("B")
                    mm(C[:, :n], PERMb, qn[:, :n], True, True, [qnb, B("cb")], [Cb])
                    t1, t1b = ftmp()
                    tt("pool", t1[:, :n], qn[:, :n], ROPE_C[:, l0:l0 + n], ALU.mult, [qnb, B("rope")] + UY, [t1b])
                    t2, t2b = ftmp()
                    tt("dve", t2[:, :n], C[:, :n], ROPE_S[:, l0:l0 + n], ALU.mult, [Cb, B("rope")] + UY, [t2b])
                    tt("pool", dest[:, t0:t0 + n], t1[:, :n], t2[:, :n], ALU.add, [t1b, t2b] + UX, [dkey(ti)])
                    yield

        def drain(gen):
            for _ in gen:
                pass

        def interleave(main, side, ratio, which=0):
            if not ILV[which] and side is not None:
                drain(side)
                side = None
            for _ in main:
                for _k in range(ratio):
                    if side is not None and next(side, "end") == "end":
                        side = None
            if side is not None:
                drain(side)

        def v_block(wv, wb, which):
            base = which * 192
            for g0 in range(0, NTT, 4):
                ng = min(4, NTT - g0)
                P, Pb = ps("O")
                for j in range(ng):
                    tk = g0 + j
                    ti = chunk_of_tile(tk)
                    for kc in range(8):
                        mm(P[:, j * 128:(j + 1) * 128], HT[:, kc, tk * 128:(tk + 1) * 128], wv[:, kc, :], kc == 0, kc == 7,
                           [wb, B("hT", kc, ti)], [Pb])
                Pv = P[:, 0:ng * 128].rearrange("p (j c) -> p j c", c=128)
                vb = [B("va", g0 + j) for j in range(ng)]
                act(VA[:, g0:g0 + ng, base:base + 64], Pv[:, :, 0:64], AF.Copy, [Pb] + UX, vb)
                act(VA[:, g0:g0 + ng, base + 128:base + 192], Pv[:, :, 64:128], AF.Copy, [Pb] + UX, vb)
                yield

        def upos(t):
            return t + 8 if t < CTX else t + 24

        def u_block(wv, wb, uc):
            for ti, (t0, n) in CHE():
                A, Ab = ps("O")
                proj_acc(A, Ab, wv, wb, ti)
                a = upos(t0)
                act(UT[:, uc, a:a + n], A[:, :n], AF.Copy, [Ab] + UY, [B("ut", uc, ti)])
                yield

        def pooling(l, uc):
            wins = (2, 4) if uc == 0 else (8, 16)
            for ti, (t0, n) in CHE():
                a = upos(t0)
                ub = [B("ut", uc, k) for k in range(max(0, ti - 1), min(NCH, ti + 2))] + UY
                b2, b2b = ftmp()
                tt("pool", b2[:, 0:n + 14], UT[:, uc, a - 8:a + n + 6], UT[:, uc, a - 7:a + n + 7], ALU.add, ub, [b2b])
                b4, b4b = ftmp()
                tt("pool", b4[:, 0:n + 12], b2[:, 0:n + 12], b2[:, 2:n + 14], ALU.add, [b2b], [b4b])
                if uc == 0:
                    srcs = ((b2, b2b, 7), (b4, b4b, 6))
                else:
                    b8, b8b = ftmp()
                    tt("pool", b8[:, 0:n + 8], b4[:, 0:n + 8], b4[:, 4:n + 12], ALU.add, [b4b], [b8b])
                    b16, b16b = ftmp()
                    tt("pool", b16[:, 0:n], b8[:, 0:n], b8[:, 8:n + 8], ALU.add, [b8b], [b16b])
                    srcs = ((b8, b8b, 4), (b16, b16b, 0))
                pl, plb = btmp()
                for h in range(2):
                    rows = slice(h * 64, h * 64 + 64)
                    bx, bxb, off = srcs[h]
                    stt(pl[rows, :n], bx[rows, off:off + n], 1.0 / wins[h], UT[rows, uc, a:a + n], ALU.mult, ALU.subtract,
                        [bxb] + ub, [plb])
                edges = []
                if ti <= 1:
                    edges.append((0, 0))
                if ti == 0 or ti == NCH - 1:
                    edges.append((n - 8, 8))
                for (c0, e0) in edges:
                    e, eb = ftmp()
                    for h in range(2):
                        rows = slice(h * 64, h * 64 + 64)
                        bx, bxb, off = srcs[h]
                        tt("dve", e[rows, 0:8], bx[rows, off + c0:off + c0 + 8], EDGE[rows, uc * 16 + e0:uc * 16 + e0 + 8],
                           ALU.mult, [bxb, B("edge")], [eb])
                    tt("dve", pl[:, c0:c0 + 8], e[:, 0:8], UT[:, uc, a + c0:a + c0 + 8], ALU.subtract, [eb, plb] + ub, [plb])
                M, Mb = ps("O")
                mm(M[:, :n], WPB[:, (l * 2 + uc) * 128:(l * 2 + uc + 1) * 128], pl[:, :n], True, True, [plb, B("wpb")], [Mb])
                ts("dve", OT[:, 3 + uc, t0:t0 + n], M[:, :n], PSC[:, l * 2 + uc:l * 2 + uc + 1], ALU.mult,
                   [Mb, B("psc")] + UX, [B("oT", 3 + uc, ti)])
                yield

        def attention(l, typ, j, qbuf):
            orow = j if typ == 0 else 5 + j
            for qi, (q0, nq) in CHE():
                if qi == 0:
                    kts = [(0, None), (1, None)]
                elif typ == 0:
                    kts = [(kt, None) for kt in range(NTT)]
                else:
                    i0 = 4 * (qi - 1)
                    kts = [(0, None), (1, None)] + [(2 + jj, jj - i0) for jj in range(max(0, i0 - 1), min(NLT, i0 + 5))]
                nk = len(kts)

                def crange(d):
                    if d is None:
                        return 0, nq
                    return max(0, 128 * (d - 1)), min(nq, 128 * (d + 2))

                strm = []
                for hp in range(2):
                    O, Ob = ps("O")
                    strm.append(dict(hp=hp, prow=slice(hp * 64, hp * 64 + 64), drow=slice(64 - hp * 64, 128 - hp * 64),
                                     head=j + 3 * hp, vbase=typ * 192 + hp * 64, O=O, Ob=Ob, pend=[]))

                def pv(sm, idx):
                    kt, pt, ptb, c0, c1 = sm["pend"][idx]
                    mm(sm["O"][:, c0:c1], VA[:, kt, sm["vbase"]:sm["vbase"] + 128], pt[:, c0:c1], idx == 0, idx == nk - 1,
                       [ptb, B("va", kt)] + UX, [sm["Ob"]])

                for idx, (kt, d) in enumerate(kts):
                    c0, c1 = crange(d)
                    for sm in strm:
                        prow = sm["prow"]
                        Sp, Sb = ps("S")
                        mm(Sp[:, c0:c1], KT[prow, typ, kt * 128:(kt + 1) * 128], QT[prow, qbuf, q0 + c0:q0 + c1], True, True,
                           [B("kT", typ, chunk_of_tile(kt)), B("qT", qbuf, qi)] + UX, [Sb])
                        pt, ptb = ptmp()
                        act(pt[:, c0:c1], Sp[:, c0:c1], AF.Exp, [Sb], [ptb], scale=0.125)
                        if d is not None:
                            tt("pool", pt[:, c0:c1], pt[:, c0:c1], MASK[:, (d + 1) * 512 + c0:(d + 1) * 512 + c1], ALU.mult,
                               [ptb, B("mask")], [ptb])
                        sm["pend"].append((kt, pt, ptb, c0, c1))
                    if idx >= 1:
                        for sm in strm:
                            pv(sm, idx - 1)
                    if idx % 4 == 3 and idx < nk - 1:
                        yield
                for sm in strm:
                    pv(sm, nk - 1)
                yield
                evs = []
                for sm in strm:
                    ob, obb = ftmp()
                    cp("dve", ob[:, :nq], sm["O"][:, :nq], [sm["Ob"]], [obb])
                    evs.append((ob, obb))
                for sm, (ob, obb) in zip(strm, evs):
                    prow, drow = sm["prow"], sm["drow"]
                    rec, recb = ftmp()
                    if typ == 1:
                        ts("dve", rec[prow, :nq], ob[drow, :nq], ES[prow, l * 6 + sm["head"]:l * 6 + sm["head"] + 1], ALU.add,
                           [obb, B("es")], [recb])
                        recip(rec[prow, :nq], rec[prow, :nq], [recb], [recb])
                    else:
                        recip(rec[prow, :nq], ob[drow, :nq], [obb], [recb])
                    tt("dve", OT[prow, orow, q0:q0 + nq], ob[prow, :nq], rec[prow, :nq], ALU.mult, [obb, recb] + UX,
                       [B("oT", orow, qi)])

        def merge(l):
            for c in range(8):
                wg = [w_get() for _ in range(3)]
                wbr = w_get()
                for ti, (t0, n) in CHE():
                    ms = []
                    for br, (r0, nr) in enumerate(((0, 3), (3, 2), (5, 3))):
                        G, Gb = ps("S")
                        proj_acc(G, Gb, wg[br][0], wg[br][1], ti)
                        sg, sgb = ftmp()
                        act(sg[:, :n], G[:, :n], AF.Sigmoid, [Gb], [sgb])
                        Bp, Bb = ps("AO")
                        for r in range(nr):
                            mm(Bp[:, :n], wbr[0][:, r0 + r, :], OT[:, r0 + r, t0:t0 + n], r == 0, r == nr - 1,
                               [wbr[1], B("oT", r0 + r, ti)] + UX, [Bb])
                        m, mb = ftmp()
                        tt("dve", m[:, :n], Bp[:, :n], sg[:, :n], ALU.mult, [Bb, sgb], [mb])
                        ms.append((m, mb))
                    s, sbf = ftmp()
                    tt("pool", s[:, :n], ms[0][0][:, :n], ms[1][0][:, :n], ALU.add, [ms[0][1], ms[1][1]], [sbf])
                    tt("pool", YT[:, c, t0:t0 + n], s[:, :n], ms[2][0][:, :n], ALU.add, [sbf, ms[2][1]] + UY,
                       [B("yT", c, ti)])

        def wout(l, b):
            join([], UX)
            for c in range(8):
                dma(XT[:, c, :], xs_d[c], UX, [B("xT", c, ti) for ti in range(NCH)])
            for c in range(8):
                wv, wb = w_get()
                for ti, (t0, n) in CHE():
                    col = NSEQ if ti == 0 else b
                    P, Pb = ps("AO")
                    for kc in range(8):
                        mm(P[:, :n], wv[:, kc, :], YT[:, kc, t0:t0 + n], kc == 0, kc == 7, [wb, B("yT", kc, ti)] + UY, [Pb])
                    stt(XT[:, c, t0:t0 + n], P[:, :n], modvec(l, 2, c, col), XT[:, c, t0:t0 + n], ALU.mult, ALU.add,
                        [Pb, B("mod"), B("xT", c, ti)] + UX, [B("xT", c, ti)])

        def mlp(l, b):
            for g in range(4):
                for kh in range(8):
                    wv, wb = w_get()
                    for ti, (t0, n) in CHE():
                        P, Pb = ps("S")
                        proj_acc(P, Pb, wv, wb, ti)
                        r, rb = ftmp()
                        act(r[:, :n], P[:, :n], AF.Relu, [Pb], [rb])
                        tt("dve", YT[:, kh, t0:t0 + n], r[:, :n], r[:, :n], ALU.mult, [rb] + UY, [B("yT", kh, ti)])
                for c in range(8):
                    wv, wb = w_get()
                    for ti, (t0, n) in CHE():
                        col = NSEQ if ti == 0 else b
                        P, Pb = ps("AO")
                        for kh in range(8):
                            mm(P[:, :n], wv[:, kh, :], YT[:, kh, t0:t0 + n], kh == 0, kh == 7, [wb, B("yT", kh, ti)] + UY, [Pb])
                        stt(XT[:, c, t0:t0 + n], P[:, :n], modvec(l, 5, c, col), XT[:, c, t0:t0 + n], ALU.mult, ALU.add,
                            [Pb, B("mod"), B("xT", c, ti)] + UX, [B("xT", c, ti)])

        def load_seq(b):
            for tk in range(NTT):
                ti = chunk_of_tile(tk)
                for half in range(2):
                    f, fb = ftmp()
                    cs = slice(half * 512, (half + 1) * 512)
                    src = ctx_d[b, tk * 128:(tk + 1) * 128, cs] if tk < 2 else x_d[b, (tk - 2) * 128:(tk - 1) * 128, cs]
                    dma(f[:, 0:512], src, [], [fb])
                    P, Pb = ps("AO")
                    for q in range(4):
                        p.op("pe", (lambda o=P[:, q * 128:(q + 1) * 128], i=f[:, q * 128:(q + 1) * 128]:
                                    nc.tensor.transpose(o, i, IDENT[:])), [fb, B("ident")], [Pb])
                    Pv = P[:].rearrange("p (j c) -> p j c", c=128)
                    cp("dve", XT[:, half * 4:half * 4 + 4, tk * 128:(tk + 1) * 128], Pv, [Pb] + UX,
                       [B("xT", half * 4 + q, ti) for q in range(4)])

        def store_seq(b):
            for lt in range(NLT):
                tk = 2 + lt
                ti = chunk_of_tile(tk)
                for half in range(2):
                    P, Pb = ps("AO")
                    for q in range(4):
                        kc = half * 4 + q
                        p.op("pe", (lambda o=P[:, q * 128:(q + 1) * 128], i=XT[:, kc, tk * 128:(tk + 1) * 128]:
                                    nc.tensor.transpose(o, i, IDENT[:])), [B("xT", kc, ti), B("ident")] + UX, [Pb])
                    f, fb = ftmp()
                    cp("dve", f[:, 0:512], P[:], [Pb], [fb])
                    dma(out_d[b, lt * 128:(lt + 1) * 128, half * 512:(half + 1) * 512], f[:, 0:512], [fb], [])

        for b in range(NSEQ):
            load_seq(b)
            for l in range(NL):
                FULL["v"] = True
                modulate(l, 0, b)
                for c in range(8):
                    dma(xs_d[c], XT[:, c, :], [B("xT", c, ti) for ti in range(NCH)] + UX, [B("spill", c)])
                join([B("spill", c) for c in range(8)] + [B("hT", kc, ti) for kc in range(8) for ti in range(NCH)], UX)
                join([], UY)
                dma(ROPE_C, rope_d[0], UY, [B("rope")])
                dma(ROPE_S, rope_d[1], UY + [B("rope")], [B("rope")])
                for uc in range(2):
                    mset("pool", UT[:, uc, :], 0.0, UY, [B("ut", uc, ti) for ti in range(NCH)])
                for tk in range(NTT):
                    mset("pool", VA[:, tk, 64:128], 1.0, UX, [B("va", tk)])
                    mset("pool", VA[:, tk, 256:320], 1.0, UX, [B("va", tk)])
                def both(*gens):
                    for g_ in gens:
                        yield from g_

                wk, wkb = w_get()
                wva, wvab = w_get()
                wvc, wvcb = w_get()
                interleave(both(v_block(wva, wvab, 0), v_block(wvc, wvcb, 1)),
                           qk_chunk(wk, wkb, l, 1, KT[:, 0, :], lambda ti: B("kT", 0, ti), full=True), 2)
                wk, wkb = w_get()
                wu0, wu0b = w_get()
                wu1, wu1b = w_get()
                FULL["v"] = l < NL - 1
                interleave(both(u_block(wu0, wu0b, 0), u_block(wu1, wu1b, 1)),
                           qk_chunk(wk, wkb, l, 3, KT[:, 1, :], lambda ti: B("kT", 1, ti), full=True), 2)
                order_q = [(typ, j) for typ in range(2) for j in range(3)]
                wv, wb = w_get()
                interleave(both(pooling(l, 0), pooling(l, 1)),
                           qk_chunk(wv, wb, l, 0, QT[:, 0, :], lambda ti: B("qT", 0, ti)), 2)
                for idx, (typ, j) in enumerate(order_q):
                    qb = idx % 2
                    side = None
                    if idx + 1 < len(order_q):
                        wv, wb = w_get()
                        nqb = (idx + 1) % 2
                        side = qk_chunk(wv, wb, l, 2 * order_q[idx + 1][0], QT[:, nqb, :], lambda ti, nqb=nqb: B("qT", nqb, ti))
                    interleave(attention(l, typ, j, qb), side, 1, 1)
                if stop_after == "attn":
                    break
                join([], UY)
                merge(l)
                if stop_after == "merge":
                    break
                wout(l, b)
                if stop_after == "wout":
                    break
                modulate(l, 1, b)
                mlp(l, b)
                if stop_after == "layer0":
                    break
            if stop_after is None:
                store_seq(b)

        p.emit(st)
    return nc


def _blk(w, rows, cols):
    sub = w[np.ix_(rows, cols)] if not isinstance(rows, slice) else w[rows][:, cols]
    return np.ascontiguousarray(sub.reshape(8, 128, 128).transpose(1, 0, 2)).reshape(128, 1024)


def _layer_blocks(l, w_in, w_br_a, w_br_b, w_br_c, w_out, w_mlp1, w_mlp2):
    ar = np.arange
    allr = slice(0, 1024)
    wi = w_in[l]
    qa0, ka0, va0, qc0, kc0, vc0, u0, ga0 = 0, 384, 512, 640, 1024, 1152, 1280, 1536
    blocks = []

    def qcols(base, j):
        return np.concatenate([base + j * 64 + ar(64), base + (j + 3) * 64 + ar(64)])

    blocks.append(_blk(wi, allr, ka0 + ar(128)))
    blocks.append(_blk(wi, allr, va0 + ar(128)))
    blocks.append(_blk(wi, allr, vc0 + ar(128)))
    blocks.append(_blk(wi, allr, kc0 + ar(128)))
    blocks.append(_blk(wi, allr, u0 + ar(128)))
    blocks.append(_blk(wi, allr, u0 + 128 + ar(128)))
    for j in range(3):
        blocks.append(_blk(wi, allr, qcols(qa0, j)))
    for j in range(3):
        blocks.append(_blk(wi, allr, qcols(qc0, j)))
    rows_a = np.concatenate([qcols(0, j) for j in range(3)])
    wbr = np.concatenate([w_br_a[l][rows_a], w_br_b[l], w_br_c[l][rows_a]], axis=0)
    for c in range(8):
        for br in range(3):
            blocks.append(_blk(wi, allr, ga0 + br * 1024 + c * 128 + ar(128)))
        blocks.append(_blk(wbr, allr, c * 128 + ar(128)))
    for c in range(8):
        blocks.append(_blk(w_out[l], allr, c * 128 + ar(128)))
    for g in range(4):
        for kh in range(8):
            blocks.append(_blk(w_mlp1[l], allr, (g * 8 + kh) * 128 + ar(128)))
        for c in range(8):
            blocks.append(_blk(w_mlp2[l], slice(g * 1024, (g + 1) * 1024), c * 128 + ar(128)))
    assert len(blocks) == NBLK_L
    return blocks


def _consts(S):
    ar = np.arange
    t = ar(S, dtype=np.float32)
    r = np.floor(t / 64.0).astype(np.float32)
    col = (t - r * 64.0).astype(np.float32)
    inv = (1.0 / (10000.0 ** (ar(0, 32, 2, dtype=np.float32) / 32.0))).astype(np.float32)
    ang = np.concatenate([r[:, None] * inv, col[:, None] * inv], axis=-1).astype(np.float32)
    cos, sin = np.cos(ang).astype(np.float32), np.sin(ang).astype(np.float32)
    d = ar(128) % 64
    C = cos[:, d // 2].T.copy()
    Sg = (sin[:, d // 2] * np.where(d % 2 == 0, -1.0, 1.0)[None, :]).T.copy()
    rope = np.stack([C, Sg]).astype(np.float32)
    k = ar(128)[:, None]
    q = ar(512)[None, :]
    mask = np.concatenate([(np.abs(q - k - 128 * dd) <= 128).astype(np.float32) for dd in range(-1, 5)], axis=1)
    perm = np.zeros((128, 128), np.float32)
    perm[ar(128) ^ 1, ar(128)] = 1.0
    bo = (ar(128)[:, None] // 64 == ar(128)[None, :] // 64).astype(np.float32)
    ones = np.ones((128, 128), np.float32)
    cb = np.concatenate([perm, bo, ones], axis=1)
    ident = np.eye(128, dtype=np.float32)
    edge = np.zeros((128, 2, 16), np.float32)
    for uc in range(2):
        for h in range(2):
            w = (2, 4, 8, 16)[uc * 2 + h]
            for i in range(8):
                edge[h * 64:(h + 1) * 64, uc, i] = 1.0 / min(w, i + w // 2)
                edge[h * 64:(h + 1) * 64, uc, 8 + i] = 1.0 / min(w, 8 - i + w // 2)
    return dict(rope=rope, mask=mask.astype(ml_dtypes.bfloat16), cb=cb.astype(ml_dtypes.bfloat16), ident=ident,
                edge=edge.reshape(128, 32))


def prep_shared(NL, S, w_ada, b_ada, norm1, norm2, w_in, q_norm_a, k_norm_a, q_norm_c, k_norm_c, sink_c, w_pool,
                pool_scale, w_br_a, w_br_b, w_br_c, w_out, w_mlp1, w_mlp2):
    ar = np.arange
    f = lambda a: np.asarray(a, dtype=np.float32)
    w_ada, b_ada, w_in, w_out, w_mlp1, w_mlp2 = f(w_ada), f(b_ada), f(w_in), f(w_out), f(w_mlp1), f(w_mlp2)
    w_br_a, w_br_b, w_br_c = f(w_br_a), f(w_br_b), f(w_br_c)
    blocks = []
    for l in range(NL):
        for m in range(48):
            blocks.append(_blk(w_ada[l], slice(0, 1024), m * 128 + ar(128)))
    for l in range(NL):
        blocks += _layer_blocks(l, w_in, w_br_a, w_br_b, w_br_c, w_out, w_mlp1, w_mlp2)
    W = np.stack(blocks)
    sh = dict(W=W)
    sh["bada"] = np.ascontiguousarray(b_ada[:NL].reshape(NL, 48, 128).transpose(2, 0, 1)).reshape(128, NL * 48)
    nrm = np.stack([f(norm1)[:NL], f(norm2)[:NL]], axis=1)
    sh["nrm"] = np.ascontiguousarray(nrm.reshape(NL, 2, 8, 128).transpose(3, 0, 1, 2)).reshape(128, NL * 16)
    g4 = np.stack([f(q_norm_a)[:NL], f(k_norm_a)[:NL], f(q_norm_c)[:NL], f(k_norm_c)[:NL]], axis=1)
    sh["qkg"] = np.ascontiguousarray(np.concatenate([g4, g4], axis=2).transpose(2, 0, 1)).reshape(128, NL * 4)
    sh["sink"] = np.ascontiguousarray(np.broadcast_to(f(sink_c)[:NL].reshape(1, NL * 6), (128, NL * 6)))
    sh["psc"] = np.ascontiguousarray(f(pool_scale)[:NL].reshape(NL, 2, 128).transpose(2, 0, 1)).reshape(128, NL * 2)
    wp = np.zeros((128, NL, 2, 128), np.float32)
    wpl = f(w_pool)
    for l in range(NL):
        for uc in range(2):
            for h in range(2):
                wp[h * 64:(h + 1) * 64, l, uc, h * 64:(h + 1) * 64] = wpl[l, uc * 2 + h]
    sh["wpool"] = wp.reshape(128, NL * 256)
    sh.update(_consts(S))
    return sh


def prep_core(c_rows, c_ctx):
    cc = np.concatenate([np.asarray(c_rows, np.float32), np.asarray(c_ctx, np.float32)[None]], axis=0)
    NBc = cc.shape[0]
    return np.ascontiguousarray(cc.reshape(NBc, 8, 128).transpose(2, 1, 0)).reshape(128, 8 * NBc)


_NC_CACHE = {}


def kernel(x, c, ctx, c_ctx, w_ada, b_ada, norm1, norm2, w_in, q_norm_a, k_norm_a, q_norm_c, k_norm_c,
           sink_c, w_pool, pool_scale, w_br_a, w_br_b, w_br_c, w_out, w_mlp1, w_mlp2):
    x = np.asarray(x, np.float32)
    c = np.asarray(c, np.float32)
    ctx = np.asarray(ctx, np.float32)
    NCORES = 8
    Bt, S, _ = x.shape
    NL = np.asarray(w_in).shape[0]
    NSEQ = Bt // NCORES
    key = (NSEQ, S, NL)
    if key not in _NC_CACHE:
        _NC_CACHE[key] = build(NSEQ, S, NL)
    nc = _NC_CACHE[key]
    sh = prep_shared(NL, S, w_ada, b_ada, norm1, norm2, w_in, q_norm_a, k_norm_a, q_norm_c, k_norm_c, sink_c, w_pool,
                     pool_scale, w_br_a, w_br_b, w_br_c, w_out, w_mlp1, w_mlp2)
    in_maps = []
    for i in range(NCORES):
        sl = slice(i * NSEQ, (i + 1) * NSEQ)
        m = dict(sh)
        m["x"] = np.ascontiguousarray(x[sl])
        m["ctx"] = np.ascontiguousarray(ctx[sl])
        m["cT"] = prep_core(c[sl], c_ctx)
        in_maps.append(m)
    res = run_bass_kernel_spmd(nc, in_maps, core_ids=list(range(NCORES)))
    return np.concatenate([r["out"] for r in res.results], axis=0).astype(np.float32)
```
